# Optimizing a Trainium2 kernel written in Bass

```python
import math
import jax, jax.numpy as jnp
from jax import lax
import numpy as np

D_MODEL = 2048
BATCH = 32
SEQ = 256
DEPTH = 2
DEC_BATCH = 4
DEC_SEQ = 1024
PAST_LEN = 512

GRID_W = 64
BLOCK = 128
EPS = 1e-6
NEG = -1e30
N_ATTN_LAYERS = (DEPTH + 1) // 2
N_SSD_LAYERS = DEPTH // 2

HD = 128
A_HEADS = 8
A_KV_HEADS = 2
A_GROUP = A_HEADS // A_KV_HEADS
WINDOW = 128
B_HEADS = 4
B_VD = 2 * HD
A_Q = A_HEADS * HD
A_KV = A_KV_HEADS * HD
B_QK = B_HEADS * 2 * HD
B_V = B_HEADS * B_VD
ATTN_IN = A_Q + 2 * A_KV + 2 * B_QK + B_V
ATTN_MIX = A_Q + B_V
ATTN_SPLITS = (A_Q, A_Q + A_KV, A_Q + 2 * A_KV, A_Q + 2 * A_KV + B_QK, A_Q + 2 * A_KV + 2 * B_QK)
ROPE_BASE = 10000.0

D_INNER = 2 * D_MODEL
SSD_HEADDIM = 64
SSD_HEADS = D_INNER // SSD_HEADDIM
SSD_GROUPS = 8
SSD_HPG = SSD_HEADS // SSD_GROUPS
SSD_STATE = 128
CONV_K = 5
CONV_CH = D_INNER + 2 * SSD_GROUPS * SSD_STATE
SSD_IN = D_INNER + CONV_CH + 2 * SSD_HEADS
CHUNK = 128

PEER_HEADS = 8
PEER_KEYS = 128
PEER_EXPERTS = PEER_KEYS * PEER_KEYS
PEER_TOPK = 16
PEER_DKEY = 256
PEER_DHALF = PEER_DKEY // 2
PEER_BLOCK = 128

kernel_name = 'hybrid_diffusion_prefix_ctx_step'


def rmsnorm(x, g):
    xf = x.astype(jnp.float32)
    y = xf * lax.rsqrt(jnp.mean(xf * xf, axis=-1, keepdims=True) + EPS)
    return (y * g.astype(jnp.float32)).astype(x.dtype)


def modulation(cond, w, b):
    return jnp.split(jax.nn.silu(cond) @ w + b, 6, axis=-1)


def adaln(x, g, shift, scale):
    return rmsnorm(x, g) * (1.0 + scale[:, None]) + shift[:, None]


def axial_rope(L, dtype):
    rows = L // GRID_W
    row = jnp.repeat(jnp.arange(rows), GRID_W).astype(jnp.float32)
    col = jnp.tile(jnp.arange(GRID_W), rows).astype(jnp.float32)
    half = HD // 2
    inv = ROPE_BASE ** (-jnp.arange(0, half, 2, dtype=jnp.float32) / half)
    ar = row[:, None] * inv
    ac = col[:, None] * inv
    ang = jnp.concatenate([ar, ar, ac, ac], axis=-1)
    return jnp.cos(ang).astype(dtype), jnp.sin(ang).astype(dtype)


def apply_rope(x, cos, sin):
    xr = x.reshape(x.shape[:-1] + (2, 2, HD // 4))
    rot = jnp.stack([-xr[..., 1, :], xr[..., 0, :]], axis=-2).reshape(x.shape)
    shp = (1, x.shape[1]) + (1,) * (x.ndim - 3) + (HD,)
    return x * cos.reshape(shp) + rot * sin.reshape(shp)


def attend(q, segs, sink=None):
    scale = q.shape[-1] ** -0.5
    logits = []
    for k, v, m in segs:
        s = jnp.einsum('bqhgd,bkhd->bhgqk', q, k, preferred_element_type=jnp.float32) * scale
        if m is not None:
            s = jnp.where(m, s, NEG)
        logits.append(s)
    if sink is not None:
        B, Tq = q.shape[:2]
        logits.append(jnp.broadcast_to(sink.astype(jnp.float32)[None, :, :, None, None], (B,) + sink.shape + (Tq, 1)))
    p = jax.nn.softmax(jnp.concatenate(logits, axis=-1), axis=-1)
    out = 0
    off = 0
    for k, v, _ in segs:
        tk = k.shape[1]
        out = out + jnp.einsum('bhgqk,bkhv->bqhgv', p[..., off:off + tk].astype(v.dtype), v)
        off += tk
    return out


def to_blocks(x):
    B, L = x.shape[:2]
    return jnp.moveaxis(x.reshape((B, L // BLOCK, BLOCK) + x.shape[2:]), 1, 0)


def from_blocks(x):
    nb, B = x.shape[:2]
    return jnp.moveaxis(x, 0, 1).reshape((B, nb * BLOCK) + x.shape[3:])


def band_windows(x):
    B, L = x.shape[:2]
    nb = L // BLOCK
    pad = [(0, 0), (BLOCK, BLOCK)] + [(0, 0)] * (x.ndim - 2)
    xb = jnp.pad(x, pad).reshape((B, nb + 2, BLOCK) + x.shape[2:])
    w = jnp.concatenate([xb[:, :-2], xb[:, 1:-1], xb[:, 2:]], axis=2)
    return jnp.moveaxis(w, 1, 0)


def split_attn_proj(p):
    B, L = p.shape[:2]
    aq, ak, av, bq, bk, bv = jnp.split(p, ATTN_SPLITS, axis=-1)
    return (aq.reshape(B, L, A_KV_HEADS, A_GROUP, HD), ak.reshape(B, L, A_KV_HEADS, HD),
            av.reshape(B, L, A_KV_HEADS, HD), bq.reshape(B, L, B_HEADS, 2, HD),
            bk.reshape(B, L, B_HEADS, 2, HD), bv.reshape(B, L, B_HEADS, B_VD))


def diff_lambda_init(layer):
    return 0.8 - 0.6 * math.exp(-0.3 * layer)


def diff_lambda(lam_vecs, lam_init):
    lv = lam_vecs.astype(jnp.float32)
    return jnp.exp(jnp.sum(lv[0] * lv[1])) - jnp.exp(jnp.sum(lv[2] * lv[3])) + lam_init


def attn_out(oa, ob, subln_g, lam_init, w_out):
    B, L = oa.shape[:2]
    ob = rmsnorm(ob.reshape(B, L, B_HEADS, B_VD), subln_g) * (1.0 - lam_init)
    mixed = jnp.concatenate([oa.reshape(B, L, A_Q), ob.reshape(B, L, B_V)], axis=-1)
    return mixed @ w_out


def attn_mix_context(h, w_in, w_out, sink, lam_vecs, subln_g, lam_init):
    aq, ak, av, bq, bk, bv = split_attn_proj(h @ w_in)
    lam = diff_lambda(lam_vecs, lam_init).astype(h.dtype)
    k1, k2 = bk[:, :, :, 0], bk[:, :, :, 1]

    def blk_a(qb):
        return attend(qb, [(ak, av, None)], sink)

    def blk_b(qs):
        q1, q2 = qs
        return attend(q1, [(k1, bv, None)]) - lam * attend(q2, [(k2, bv, None)])

    oa = from_blocks(lax.map(blk_a, to_blocks(aq)))
    ob = from_blocks(lax.map(blk_b, (to_blocks(bq[:, :, :, 0:1]), to_blocks(bq[:, :, :, 1:2]))))
    return attn_out(oa, ob, subln_g, lam_init, w_out), (ak, av, bk, bv)


def attn_mix_latent(h, ck_a, cv_a, ck_b, cv_b, w_in, w_out, sink, lam_vecs, subln_g, lam_init):
    B, L = h.shape[:2]
    aq, ak, av, bq, bk, bv = split_attn_proj(h @ w_in)
    cos, sin = axial_rope(L, h.dtype)
    aq, ak, bq, bk = (apply_rope(t, cos, sin) for t in (aq, ak, bq, bk))
    lam = diff_lambda(lam_vecs, lam_init).astype(h.dtype)
    nb = L // BLOCK
    qpos = jnp.arange(BLOCK)[:, None]
    kpos = jnp.arange(3 * BLOCK)[None, :]
    rel = kpos - BLOCK - qpos

    def blk_a(args):
        qb, kb, vb, i = args
        s = i * BLOCK + kpos - BLOCK
        m = (jnp.abs(rel) <= WINDOW) & (s >= 0) & (s < L)
        return attend(qb, [(ck_a, cv_a, None), (kb, vb, m)], sink)

    ck1, ck2 = ck_b[:, :, :, 0], ck_b[:, :, :, 1]
    k1, k2 = bk[:, :, :, 0], bk[:, :, :, 1]

    def blk_b(qs):
        q1, q2 = qs
        return (attend(q1, [(ck1, cv_b, None), (k1, bv, None)])
                - lam * attend(q2, [(ck2, cv_b, None), (k2, bv, None)]))

    oa = from_blocks(lax.map(blk_a, (to_blocks(aq), band_windows(ak), band_windows(av), jnp.arange(nb))))
    ob = from_blocks(lax.map(blk_b, (to_blocks(bq[:, :, :, 0:1]), to_blocks(bq[:, :, :, 1:2]))))
    return attn_out(oa, ob, subln_g, lam_init, w_out)


def dwconv(x, w, b):
    y = lax.conv_general_dilated(x, w[:, None, :], window_strides=(1,), padding=[(CONV_K // 2, CONV_K // 2)],
                                 dimension_numbers=('NWC', 'WIO', 'NWC'), feature_group_count=x.shape[-1])
    return y + b


def ssd_scan(x, dt, a, bm, cm, h0):
    f32 = jnp.float32
    B, L = x.shape[:2]
    nc = L // CHUNK
    cshape = (B, nc, CHUNK, SSD_GROUPS, SSD_HPG)
    dtc = dt.reshape(cshape)
    ac = jnp.cumsum(dtc * a.reshape(SSD_GROUPS, SSD_HPG), axis=2)
    dtx = x.astype(f32).reshape(cshape + (SSD_HEADDIM,)) * dtc[..., None]
    bc = bm.astype(f32).reshape(B, nc, CHUNK, SSD_GROUPS, SSD_STATE)
    cc = cm.astype(f32).reshape(B, nc, CHUNK, SSD_GROUPS, SSD_STATE)
    causal = jnp.tril(jnp.ones((CHUNK, CHUNK), bool))[:, :, None, None]
    decay_in = jnp.exp(jnp.where(causal, ac[:, :, :, None] - ac[:, :, None, :], NEG))
    scores = jnp.einsum('bclgn,bcsgn->bclsg', cc, bc)
    y_diag = jnp.einsum('bclsgh,bcsghp->bclghp', scores[..., None] * decay_in, dtx)
    decay_out = jnp.exp(ac[:, :, -1:] - ac)
    states = jnp.einsum('bclgn,bclghp->bcghpn', bc, dtx * decay_out[..., None])
    chunk_decay = jnp.exp(ac[:, :, -1])

    def step(hs, inp):
        st, dec = inp
        return dec[..., None, None] * hs + st, hs

    h_init = h0.astype(f32).reshape(B, SSD_GROUPS, SSD_HPG, SSD_HEADDIM, SSD_STATE)
    h_last, h_prev = lax.scan(step, h_init, (jnp.moveaxis(states, 1, 0), jnp.moveaxis(chunk_decay, 1, 0)))
    h_in = jnp.moveaxis(h_prev, 0, 1)
    y_off = jnp.einsum('bclgn,bcghpn->bclghp', cc, h_in) * jnp.exp(ac)[..., None]
    y = (y_diag + y_off).reshape(B, L, SSD_HEADS, SSD_HEADDIM)
    return y.astype(x.dtype), h_last.reshape(B, SSD_HEADS, SSD_HEADDIM, SSD_STATE).astype(x.dtype)


def ssd_mix(h, h0_fwd, h0_bwd, w_in, conv_w, conv_b, dt_bias, a_log, d_skip, norm_g, w_out):
    B, L = h.shape[:2]
    z, xbc, dt = jnp.split(h @ w_in, (D_INNER, D_INNER + CONV_CH), axis=-1)
    xbc = jax.nn.silu(dwconv(xbc, conv_w, conv_b))
    xs, bm, cm = jnp.split(xbc, (D_INNER, D_INNER + SSD_GROUPS * SSD_STATE), axis=-1)
    xs = xs.reshape(B, L, SSD_HEADS, SSD_HEADDIM)
    bm = bm.reshape(B, L, SSD_GROUPS, SSD_STATE)
    cm = cm.reshape(B, L, SSD_GROUPS, SSD_STATE)
    dt = jax.nn.softplus(dt.astype(jnp.float32).reshape(B, L, 2, SSD_HEADS) + dt_bias.astype(jnp.float32))
    a = -jnp.exp(a_log.astype(jnp.float32))
    y_f, h_f = ssd_scan(xs, dt[:, :, 0], a[0], bm, cm, h0_fwd)
    flip = lambda t: jnp.flip(t, axis=1)
    y_b, h_b = ssd_scan(flip(xs), flip(dt[:, :, 1]), a[1], flip(bm), flip(cm), h0_bwd)
    y = y_f + flip(y_b) + d_skip[:, None] * xs
    gs = D_INNER // SSD_GROUPS
    y = (y.reshape(B, L, D_INNER) * jax.nn.silu(z)).reshape(B, L, SSD_GROUPS, gs)
    y = rmsnorm(y, norm_g.reshape(SSD_GROUPS, gs)).reshape(B, L, D_INNER)
    return y @ w_out, (h_f, h_b)


def peer(h, w_q, sub_keys, u, v):
    B, L, D = h.shape

    def blk(xb):
        q = (xb @ w_q).reshape(PEER_BLOCK, PEER_HEADS, 2, PEER_DHALF)
        s = jnp.einsum('thcd,hckd->thck', q, sub_keys, preferred_element_type=jnp.float32)
        sv, si = lax.top_k(s, PEER_TOPK)
        cand = (sv[:, :, 0, :, None] + sv[:, :, 1, None, :]).reshape(PEER_BLOCK, PEER_HEADS, PEER_TOPK * PEER_TOPK)
        cv, ci = lax.top_k(cand, PEER_TOPK)
        i1 = jnp.take_along_axis(si[:, :, 0], ci // PEER_TOPK, axis=-1)
        i2 = jnp.take_along_axis(si[:, :, 1], ci % PEER_TOPK, axis=-1)
        idx = i1 * PEER_KEYS + i2
        g = jax.nn.softmax(cv, axis=-1)
        act = jax.nn.gelu(jnp.einsum('td,thkd->thk', xb, u[idx]))
        return jnp.einsum('thk,thkd->td', (g * act).astype(xb.dtype), v[idx])

    return lax.map(blk, h.reshape(-1, PEER_BLOCK, D)).reshape(B, L, D)


def setup_inputs(seed: int = 0) -> dict:
    key = jax.random.key(seed)
    ks = iter(jax.random.split(key, 48))

    def nrm(shape, scale=1.0):
        return jax.random.normal(next(ks), shape, jnp.float32) * scale

    NA, NS = N_ATTN_LAYERS, N_SSD_LAYERS
    dt0 = jnp.exp(jax.random.uniform(next(ks), (NS, 2, SSD_HEADS), jnp.float32, math.log(1e-3), math.log(1e-1)))
    a0 = jax.random.uniform(next(ks), (NS, 2, SSD_HEADS), jnp.float32, 1.0, 16.0)
    return {
        'x_prompt': nrm((BATCH, SEQ, D_MODEL)),
        'x_sample': nrm((DEC_BATCH, DEC_SEQ, D_MODEL)),
        'cache_a_k': nrm((DEC_BATCH, NA, PAST_LEN, A_KV_HEADS, HD)),
        'cache_a_v': nrm((DEC_BATCH, NA, PAST_LEN, A_KV_HEADS, HD)),
        'cache_b_k': nrm((DEC_BATCH, NA, PAST_LEN, B_HEADS, 2, HD)),
        'cache_b_v': nrm((DEC_BATCH, NA, PAST_LEN, B_HEADS, B_VD)),
        'state_ssd_fwd': nrm((DEC_BATCH, NS, SSD_HEADS, SSD_HEADDIM, SSD_STATE), 0.1),
        'state_ssd_bwd': nrm((DEC_BATCH, NS, SSD_HEADS, SSD_HEADDIM, SSD_STATE), 0.1),
        'c': nrm((DEC_BATCH, D_MODEL)),
        'c_ctx': nrm((D_MODEL,)),
        'w_mod': nrm((DEPTH, D_MODEL, 6 * D_MODEL), D_MODEL ** -0.5),
        'b_mod': nrm((DEPTH, 6 * D_MODEL), 0.02),
        'norm_g': 1.0 + nrm((DEPTH, 2, D_MODEL), 0.02),
        'attn_w_in': nrm((NA, D_MODEL, ATTN_IN), D_MODEL ** -0.5),
        'attn_w_out': nrm((NA, ATTN_MIX, D_MODEL), ATTN_MIX ** -0.5),
        'attn_sink': nrm((NA, A_KV_HEADS, A_GROUP), 0.5),
        'diff_lam': nrm((NA, 4, HD), 0.1),
        'diff_subln_g': 1.0 + nrm((NA, B_VD), 0.02),
        'ssd_w_in': nrm((NS, D_MODEL, SSD_IN), D_MODEL ** -0.5),
        'ssd_conv_w': nrm((NS, CONV_K, CONV_CH), CONV_K ** -0.5),
        'ssd_conv_b': nrm((NS, CONV_CH), 0.02),
        'ssd_dt_bias': dt0 + jnp.log(-jnp.expm1(-dt0)),
        'ssd_a_log': jnp.log(a0),
        'ssd_d': 1.0 + nrm((NS, SSD_HEADS), 0.02),
        'ssd_norm_g': 1.0 + nrm((NS, D_INNER), 0.02),
        'ssd_w_out': nrm((NS, D_INNER, D_MODEL), D_INNER ** -0.5),
        'peer_w_q': nrm((DEPTH, D_MODEL, PEER_HEADS * PEER_DKEY), D_MODEL ** -0.5),
        'peer_sub_keys': nrm((DEPTH, PEER_HEADS, 2, PEER_KEYS, PEER_DHALF), PEER_DHALF ** -0.5),
        'peer_u': nrm((DEPTH, PEER_EXPERTS, D_MODEL), D_MODEL ** -0.5),
        'peer_v': nrm((DEPTH, PEER_EXPERTS, D_MODEL), (PEER_HEADS * PEER_TOPK) ** -0.5),
        'final_g': 1.0 + nrm((D_MODEL,), 0.02),
    }


def reference(x_prompt, x_sample, cache_a_k, cache_a_v, cache_b_k, cache_b_v, state_ssd_fwd, state_ssd_bwd,
              c, c_ctx, w_mod, b_mod, norm_g, attn_w_in, attn_w_out, attn_sink, diff_lam, diff_subln_g,
              ssd_w_in, ssd_conv_w, ssd_conv_b, ssd_dt_bias, ssd_a_log, ssd_d, ssd_norm_g, ssd_w_out,
              peer_w_q, peer_sub_keys, peer_u, peer_v, final_g):
    x = x_prompt
    Bp = x.shape[0]
    ak_l, av_l, bk_l, bv_l, sf_l, sb_l = [], [], [], [], [], []
    for l in range(DEPTH):
        i = l // 2
        sh1, sc1, g1, sh2, sc2, g2 = modulation(c_ctx[None], w_mod[l], b_mod[l])
        h = adaln(x, norm_g[l, 0], sh1, sc1)
        if l % 2 == 0:
            out, (ak, av, bk, bv) = attn_mix_context(h, attn_w_in[i], attn_w_out[i], attn_sink[i], diff_lam[i],
                                                     diff_subln_g[i], diff_lambda_init(l))
            ak_l.append(ak); av_l.append(av); bk_l.append(bk); bv_l.append(bv)
        else:
            h0 = jnp.zeros((Bp, SSD_HEADS, SSD_HEADDIM, SSD_STATE), jnp.float32)
            out, (sf, sb) = ssd_mix(h, h0, h0, ssd_w_in[i], ssd_conv_w[i], ssd_conv_b[i], ssd_dt_bias[i],
                                    ssd_a_log[i], ssd_d[i], ssd_norm_g[i], ssd_w_out[i])
            sf_l.append(sf); sb_l.append(sb)
        x = x + g1[:, None] * out
        x = x + g2[:, None] * peer(adaln(x, norm_g[l, 1], sh2, sc2), peer_w_q[l], peer_sub_keys[l], peer_u[l], peer_v[l])
    y_prompt = rmsnorm(x, final_g)

    x = x_sample
    for l in range(DEPTH):
        i = l // 2
        sh1, sc1, g1, sh2, sc2, g2 = modulation(c, w_mod[l], b_mod[l])
        h = adaln(x, norm_g[l, 0], sh1, sc1)
        if l % 2 == 0:
            out = attn_mix_latent(h, cache_a_k[:, i], cache_a_v[:, i], cache_b_k[:, i], cache_b_v[:, i],
                                  attn_w_in[i], attn_w_out[i], attn_sink[i], diff_lam[i], diff_subln_g[i],
                                  diff_lambda_init(l))
        else:
            out, _ = ssd_mix(h, state_ssd_fwd[:, i], state_ssd_bwd[:, i], ssd_w_in[i], ssd_conv_w[i], ssd_conv_b[i],
                             ssd_dt_bias[i], ssd_a_log[i], ssd_d[i], ssd_norm_g[i], ssd_w_out[i])
        x = x + g1[:, None] * out
        x = x + g2[:, None] * peer(adaln(x, norm_g[l, 1], sh2, sc2), peer_w_q[l], peer_sub_keys[l], peer_u[l], peer_v[l])
    y_sample = rmsnorm(x, final_g)

    new_a_k = jnp.stack(ak_l, axis=1)
    new_a_v = jnp.stack(av_l, axis=1)
    new_b_k = jnp.stack(bk_l, axis=1)
    new_b_v = jnp.stack(bv_l, axis=1)
    new_ssd_fwd = jnp.stack(sf_l, axis=1)
    new_ssd_bwd = jnp.stack(sb_l, axis=1)
    return (y_prompt, y_sample, new_a_k, new_a_v, new_b_k, new_b_v, new_ssd_fwd, new_ssd_bwd)
```

```python
import math
import numpy as np
import concourse.bass as bass
import concourse.mybir as mybir
from concourse.bass_utils import run_bass_kernel_spmd
from contextlib import ExitStack

F32 = mybir.dt.float32
BF16 = mybir.dt.bfloat16
AF = mybir.ActivationFunctionType
ALU = mybir.AluOpType
AX = mybir.AxisListType

D = 2048
NCH = 16
NT = 1024
EPS = 1e-6
NEG = -1.0e30
PEER_E = 16384

ENGS = ("pe", "act", "dve", "pool", "sp")


class Buf:
    __slots__ = ("ap", "name", "wr", "rd", "dsem", "dcnt", "multi")

    def __init__(self, ap, name, multi=False):
        self.ap = ap
        self.name = name
        self.wr = []
        self.rd = []
        self.dsem = None
        self.dcnt = 0
        self.multi = multi

    def __getitem__(self, idx):
        return View(self, self.ap[idx])


class View:
    __slots__ = ("buf", "ap")

    def __init__(self, buf, ap):
        self.buf = buf
        self.ap = ap

    def __getitem__(self, idx):
        return View(self.buf, self.ap[idx])


def _b(x):
    return x.buf if isinstance(x, View) else x


def A(x):
    if isinstance(x, (View, Buf)):
        return x.ap
    return x


class Prog:
    def __init__(self, nc):
        self.nc = nc
        self.es = ExitStack()
        self.q = {e: [] for e in ENGS}
        self.sems = {}
        self.cnt = {e: 0 for e in ENGS}
        self.seen = {e: {} for e in ENGS}
        self.pending = {e: False for e in ENGS}
        self.out_tokens = []
        self.nbuf = 0
        for e in ("pe", "act", "dve", "pool"):
            self.sems[e] = self.es.enter_context(nc.semaphore("s_" + e))

    def sbuf(self, name, shape, dt):
        t = self.es.enter_context(self.nc.sbuf_tensor(name, list(shape), dt))
        return Buf(t[:], name)

    def psum(self, name, shape, dt=F32):
        t = self.es.enter_context(self.nc.psum_tensor(name, list(shape), dt))
        return Buf(t[:], name)

    def sub(self, view, name="sub"):
        return Buf(A(view), name)

    def alias(self, new, olds):
        for o in olds:
            new.rd.extend(o.wr)
            new.rd.extend(o.rd)
        new.rd = self._maxtoks(new.rd)
        return new

    def dsem(self, buf):
        if buf.dsem is None:
            key = "d%d" % self.nbuf
            self.nbuf += 1
            self.sems[key] = self.es.enter_context(self.nc.semaphore(key))
            buf.dsem = key
        return buf.dsem

    def _need(self, eng, tok, waits):
        if tok is None:
            return
        k, v = tok
        if k == "pe" and eng == "pe":
            return
        if self.seen[eng].get(k, 0) >= v:
            return
        if waits.get(k, 0) < v:
            waits[k] = v

    def _deps(self, eng, reads, writes):
        waits = {}
        for r in reads:
            for t in _b(r).wr:
                self._need(eng, t, waits)
        for w in writes:
            b = _b(w)
            for t in b.wr:
                self._need(eng, t, waits)
            for t in b.rd:
                self._need(eng, t, waits)
        for k, v in waits.items():
            self.seen[eng][k] = v
        return list(waits.items())

    def op(self, eng, fn, reads=(), writes=(), inc=True):
        reads = [r for r in reads if isinstance(r, (Buf, View))]
        writes = [w for w in writes if isinstance(w, (Buf, View))]
        waits = self._deps(eng, reads, writes)
        tick = self.cnt[eng] + 1
        if inc:
            self.cnt[eng] = tick
            self.pending[eng] = False
        else:
            self.pending[eng] = True
        tok = (eng, tick)
        for r in reads:
            b = _b(r)
            if len(b.rd) > 64:
                b.rd = b.rd[-48:] + self._maxtoks(b.rd[:-48])
            b.rd.append(tok)
        for w in writes:
            b = _b(w)
            b.wr = [tok]
            b.rd = []
        self.q[eng].append((waits, fn, (eng, 1) if inc else None))

    @staticmethod
    def _maxtoks(toks):
        m = {}
        for k, v in toks:
            if m.get(k, 0) < v:
                m[k] = v
        return list(m.items())

    def dma(self, eng, out, in_, **kw):
        ob, ib = _b(out), _b(in_)
        if isinstance(ob, Buf) and not ob.multi:
            tb = ob
        else:
            tb = ib
        key = self.dsem(tb)
        waits = self._deps(eng, [ib] if isinstance(ib, Buf) else [], [ob] if isinstance(ob, Buf) else [])
        tb.dcnt += 16
        tok = (key, tb.dcnt)
        if isinstance(ib, Buf):
            ib.rd.append(tok)
        if isinstance(ob, Buf):
            if ob.multi:
                ob.wr = self._maxtoks(ob.wr + [tok])
            else:
                ob.wr = [tok]
                ob.rd = []
        oa, ia = A(out), A(in_)
        self.q[eng].append((waits, lambda e: e.dma_start(out=oa, in_=ia, **kw), (key, 16)))
        return tok

    def emit(self):
        nc = self.nc
        final_waits = {}
        for k, v in self.out_tokens:
            final_waits[k] = max(final_waits.get(k, 0), v)
        for e in ("pe", "act", "dve", "pool"):
            assert not self.pending[e], "engine %s ends with a non-inc instruction" % e
            if self.cnt[e]:
                final_waits[e] = self.cnt[e]
        self.q["sp"].append((list(final_waits.items()), None, None))
        sems, q = self.sems, self.q

        def run(engname):
            def body(e):
                for waits, fn, inc in q[engname]:
                    for k, v in waits:
                        e.wait_ge(sems[k], v)
                    if fn is None:
                        continue
                    ins = fn(e)
                    if inc is not None:
                        ins.then_inc(sems[inc[0]], inc[1])
            return body

        with nc.Block() as block:
            block.tensor(run("pe"))
            block.scalar(run("act"))
            block.vector(run("dve"))
            block.gpsimd(run("pool"))
            block.sync(run("sp"))
        self.es.close()


class K:
    def __init__(self, nc, flags):
        self.nc = nc
        self.p = Prog(nc)
        self.flags = flags
        self.ins = {}
        self.outs = {}
        self.pb_i = 0

    def inp(self, name, shape):
        self.ins[name] = self.nc.dram_tensor(name, list(shape), F32, kind="ExternalInput").ap()
        return self.ins[name]

    def outp(self, name, shape):
        self.outs[name] = self.nc.dram_tensor(name, list(shape), F32, kind="ExternalOutput").ap()
        return self.outs[name]

    def mm(self, ps, lhsT, rhs, start, stop, inc=None):
        if inc is None:
            inc = stop
        pa, la, ra = A(ps), A(lhsT), A(rhs)
        self.p.op("pe", lambda e: e.matmul(pa, lhsT=la, rhs=ra, start=start, stop=stop),
                  reads=[lhsT, rhs], writes=[ps], inc=inc)

    def act(self, out, in_, func, bias=None, scale=None, reads=(), accum_out=None):
        oa, ia = A(out), A(in_)
        kw = {}
        if bias is not None:
            kw["bias"] = A(bias)
        if scale is not None:
            kw["scale"] = A(scale)
        if accum_out is not None:
            kw["accum_out"] = A(accum_out)
        w = [out] + ([accum_out] if accum_out is not None else [])
        self.p.op("act", lambda e: e.activation(out=oa, in_=ia, func=func, **kw),
                  reads=[in_, bias, scale] + list(reads), writes=w)

    def tt(self, out, in0, in1, op, eng="dve"):
        oa, a0, a1 = A(out), A(in0), A(in1)
        self.p.op(eng, lambda e: e.tensor_tensor(out=oa, in0=a0, in1=a1, op=op), reads=[in0, in1], writes=[out])

    def ts(self, out, in0, s1, op0, s2=None, op1=None, eng="dve"):
        oa, a0 = A(out), A(in0)
        s1a, s2a = A(s1), A(s2)
        if op1 is None:
            fn = lambda e: e.tensor_scalar(out=oa, in0=a0, scalar1=s1a, scalar2=None, op0=op0)
        else:
            fn = lambda e: e.tensor_scalar(out=oa, in0=a0, scalar1=s1a, scalar2=s2a, op0=op0, op1=op1)
        self.p.op(eng, fn, reads=[in0, s1, s2], writes=[out])

    def stt(self, out, in0, scalar, in1, op0, op1):
        oa, a0, sa, a1 = A(out), A(in0), A(scalar), A(in1)
        self.p.op("dve", lambda e: e.scalar_tensor_tensor(out=oa, in0=a0, scalar=sa, in1=a1, op0=op0, op1=op1),
                  reads=[in0, scalar, in1], writes=[out])

    def copy(self, out, in_, eng="dve"):
        oa, ia = A(out), A(in_)
        if eng == "act":
            self.p.op("act", lambda e: e.copy(out=oa, in_=ia), reads=[in_], writes=[out])
        else:
            self.p.op(eng, lambda e: e.tensor_copy(out=oa, in_=ia), reads=[in_], writes=[out])

    def memset(self, out, val, eng="dve"):
        oa = A(out)
        self.p.op(eng, lambda e: e.memset(oa, val), reads=[], writes=[out])

    def bank(self):
        b = self.pb[self.pb_i % len(self.pb)]
        self.pb_i += 1
        return b


def bc(ap, shape, axis):
    idx = [slice(None)] * len(ap.shape)
    idx.insert(axis, None)
    return ap[tuple(idx)].to_broadcast(list(shape))


def build_program(flags):
    nc = bass.Bass("TRN2", target_bir_lowering=False)
    k = K(nc, flags)
    p = k.p

    has_peer = flags.get("pphase", 9) > 0 and flags.get("peer", True)
    x_in = [k.inp("xp", [128, NCH, NT]), k.inp("xs", [128, NCH, NT])]
    y_out = [k.outp("yp", [128, NCH, NT]), k.outp("ys", [128, NCH, NT])]
    cond = k.inp("cond", [128, NCH, 2])
    w_mod = k.inp("w_mod", [2, D, 6 * D]) if flags.get("dbg", 9) >= 2 else None
    b_mod = k.inp("b_mod", [128, 2, 96])
    ng = k.inp("ng", [128, 4, NCH])
    fg = k.inp("fg", [128, NCH])
    ident_in = k.inp("ident", [128, 128])
    w_q = k.inp("w_q", [2, D, D]) if has_peer else None
    skT = k.inp("skT", [2, 128, 16, 128]) if has_peer else None
    uT = k.inp("uT", [2, D, PEER_E]) if has_peer else None
    pv = k.inp("pv", [2, PEER_E, D]) if has_peer else None
    has_peer = flags.get("pphase", 9) > 0 and flags.get("peer", True)
    if has_peer:
        Gd = nc.dram_tensor("Gd", [NT, PEER_E], BF16, kind="Internal").ap()
        Gd_b = Buf(Gd, "Gd", multi=True)

    has_attn = flags.get("attn", True)
    if has_attn:
        w_in = k.inp("attn_w_in", [D, 4608])
        w_out = k.inp("attn_w_out", [D, D])
        esink_in = k.inp("esink", [128, 8])
        lamv_in = k.inp("lamv", [128, 4, 128])
        subg_in = k.inp("subg", [128, 2])
        rope_in = k.inp("rope", [2, 128, NT])
        rperm_in = k.inp("rperm", [128, 128])
        masks_in = k.inp("masks", [2, 128, 512])
        caK_in = k.inp("caK", [128, 2, 512])
        caV_in = k.inp("caV", [128, 4, 2, 128])
        cbK_in = k.inp("cbK", [128, 4, 2, 512])
        cbV_in = k.inp("cbV", [128, 4, 4, 256])
        na_k = k.outp("na_k", [NT, 256])
        na_v = k.outp("na_v", [NT, 256])
        nb_k = k.outp("nb_k", [NT, 1024])
        nb_v = k.outp("nb_v", [NT, 1024])

    has_ssd = flags.get("ssd", True)
    if has_ssd:
        sw_in = k.inp("ssd_w_in", [D, 10368])
        sw_out = k.inp("ssd_w_out", [4096, D])
        convw_in = k.inp("convw", [128, 48, 6])
        dtb_in = k.inp("dtbias", [128, 128])
        alog_in = k.inp("alog", [128, 128])
        dsk_in = k.inp("dskip", [128, 64])
        ngrp_in = k.inp("ngrp", [128, 32])
        tri_in = k.inp("tri", [4, 128, 128])
        st_in = [k.inp("st_f", [128, 64, 64]), k.inp("st_b", [128, 64, 64])]
        ns_out = [k.outp("ns_f", [4 * 64 * 64, 128]), k.outp("ns_b", [4 * 64 * 64, 128])]

    xT = p.sbuf("xT", [128, NCH, NT], F32)
    hT = p.sbuf("hT", [128, NCH, NT], BF16)
    arA = p.sbuf("arA", [128, 3 * 8192], BF16)
    arB = p.sbuf("arB", [128, 20480], BF16)
    modT = p.sbuf("modT", [128, 2, 96, 2], F32)
    bmod = p.sbuf("bmod", [128, 2, 96], F32)
    ngs = p.sbuf("ngs", [128, 4, NCH], F32)
    fgs = p.sbuf("fgs", [128, NCH], F32)
    gm = p.sbuf("gm", [128, 2, 2, 2, NCH], F32)
    scond = p.sbuf("scond", [128, NCH, 2], BF16)
    condf = p.sbuf("condf", [128, NCH, 2], F32)
    ones_bf = p.sbuf("ones_bf", [128, 128], BF16)
    ident_bf = p.sbuf("ident_bf", [128, 128], BF16)
    epsb = p.sbuf("epsb", [128, 1], F32)
    rstd = p.sbuf("rstd", [128, 512], F32)
    tmpx = [p.sbuf("tmpx%d" % i, [128, 512], F32) for i in range(2)]
    arC = p.sbuf("arC", [128, 4608], BF16)

    if has_attn:
        esink = p.sbuf("esink_sb", [128, 8], F32)
        lamv = View(tmpx[1], tmpx[1].ap.rearrange("p (a d) -> p a d", a=4))
        lamt = p.sbuf("lamt", [128, 8], F32)
        subgl = p.sbuf("subgl", [128, 2], F32)
        rt1, rt2 = tmpx[0], tmpx[1]
        stage = [rstd[:, 0:256], rstd[:, 256:512]]
    if has_ssd:
        convw = p.sbuf("convw_sb", [128, 48, 6], F32)
        dtb = p.sbuf("dtb_sb", [128, 128], F32)
        ABt = p.sbuf("ABt", [128, 128], F32)
        dsk = p.sbuf("dsk_sb", [128, 64], F32)
        ngrp = p.sbuf("ngrp_sb", [128, 32], F32)
        maskn = p.sbuf("maskn", [128, 2, 128], F32)
        identF = p.sbuf("identF", [128, 128], F32)
        onesF = p.sbuf("onesF", [128, 128], F32)
        GTs = p.sbuf("GTs", [128, 128], BF16)
    wslot = [p.sub(arA[:, i * 8192:(i + 1) * 8192], "wslot%d" % i) for i in range(3)]
    k.ws_i = 0

    def wget():
        b = wslot[k.ws_i % 3]
        k.ws_i += 1
        return b

    k.pb = [p.psum("pb%d" % i, [128, 512]) for i in range(7)]
    ptr = p.psum("ptr", [128, 1024], BF16)

    k.memset(ones_bf, 1.0)
    k.memset(epsb, EPS)
    p.dma("pool", ident_bf, ident_in)
    p.dma("sp", bmod, b_mod)
    p.dma("sp", ngs, ng)
    p.dma("sp", fgs, fg)
    p.dma("sp", condf, cond)

    if has_attn:
        p.dma("sp", esink, esink_in)
        p.dma("sp", lamv, lamv_in)
        p.dma("sp", subgl, subg_in)
        k.act(esink, esink, AF.Exp)
        LAM_INIT = 0.8 - 0.6 * math.exp(-0.3 * 0)
        for j in range(2):
            k.tt(rt1[:, 0:128], lamv[:, 2 * j, :], lamv[:, 2 * j + 1, :], ALU.mult)
            r1, lo = A(rt1[:, 0:128]), A(lamt[:, j:j + 1])
            p.op("dve", lambda e, r1=r1, lo=lo: e.reduce_sum(out=lo, in_=r1, axis=AX.X), reads=[rt1], writes=[lamt])
        k.act(lamt[:, 0:2], lamt[:, 0:2], AF.Exp)
        k.tt(lamt[:, 2:3], lamt[:, 1:2], lamt[:, 0:1], ALU.subtract)
        k.ts(lamt[:, 3:4], lamt[:, 2:3], -LAM_INIT, ALU.add)
        k.ts(subgl, subgl, 1.0 - LAM_INIT, ALU.mult)

    if has_ssd:
        p.dma("sp", convw, convw_in)
        p.dma("sp", dtb, dtb_in)
        p.dma("sp", ABt, alog_in)
        p.dma("sp", dsk, dsk_in)
        p.dma("sp", ngrp, ngrp_in)
        p.dma("sp", maskn, tri_in[2:4].rearrange("a p t -> p a t"))
        p.dma("sp", identF, ident_in)
        k.memset(onesF, 1.0)
        k.act(ABt, ABt, AF.Exp)
        k.ts(ABt, ABt, -1.0, ALU.mult)

    if "silu" not in flags.get("skip", ()):
        k.act(scond, condf, AF.Silu)
    for l in range(2):
        if flags.get("dbg", 9) < 2:
            break
        pm = k.bank()
        for blk in range(24):
            w = wget()
            wv = w.ap.rearrange("p (c j) -> p c j", c=16)
            p.dma("pool", View(w, wv), w_mod[l, :, blk * 512:(blk + 1) * 512].rearrange("(c p) j -> p c j", p=128))
            for jb in range(4):
                jc = blk * 4 + jb
                for dc in range(NCH):
                    k.mm(View(pm, pm.ap[:, jc * 2:jc * 2 + 2]), View(w, wv[:, dc, jb * 128:(jb + 1) * 128]),
                         scond[:, dc, :], start=(dc == 0), stop=(dc == NCH - 1))
        pmv = pm.ap[:, 0:192].rearrange("p (j c) -> p j c", c=2)
        k.tt(modT[:, l], View(pm, pmv), View(bmod, bc(bmod.ap[:, l], [128, 96, 2], 2)), ALU.add)
    for ci in range(2):
        if flags.get("dbg", 9) < 2:
            break
        for l in range(2):
            for kk in range(2):
                sc = modT[:, l, 16 + 48 * kk:32 + 48 * kk, ci]
                k.stt(gm[:, ci, l, kk], sc, 1.0, ngs[:, l * 2 + kk], ALU.add, ALU.mult)

    def mod_vec(ci, l, which):
        return modT[:, l, which * 16:(which + 1) * 16, ci]

    def rms_stats(blk):
        sq = p.sub(arB[:, 0:8192], "sq")
        p.alias(sq, [arB])
        sqv = View(sq, sq.ap.rearrange("p (c t) -> p c t", c=NCH))
        k.act(sqv, xT[:, :, blk * 512:(blk + 1) * 512], AF.Square)
        ps = k.bank()
        for dc in range(NCH):
            k.mm(ps, ones_bf, sqv[:, dc, :], start=(dc == 0), stop=(dc == NCH - 1))
        p.alias(arB, [sq])
        if "sqrt" in flags.get("skip", ()):
            k.copy(rstd, ps)
        else:
            k.act(rstd, ps, AF.Sqrt, bias=epsb, scale=1.0 / D)
        rs = A(rstd)
        if "recip" not in flags.get("skip", ()):
            p.op("dve", lambda e: e.reciprocal(out=rs, in_=rs), reads=[rstd], writes=[rstd])

    def adaln(ci, l, kk):
        sh = mod_vec(ci, l, 3 * kk)
        for blk in range(2):
            rms_stats(blk)
            for dc in range(NCH):
                t = tmpx[dc % 2]
                k.stt(t, xT[:, dc, blk * 512:(blk + 1) * 512], gm[:, ci, l, kk, dc:dc + 1], rstd, ALU.mult, ALU.mult)
                k.act(hT[:, dc, blk * 512:(blk + 1) * 512], t, AF.Identity, bias=sh[:, dc:dc + 1])

    def final_norm(dst):
        sk = flags.get("skip", ())
        for blk in range(2):
            if "rms" not in sk:
                rms_stats(blk)
            for dc in range(NCH):
                t = tmpx[dc % 2]
                if "stt" in sk:
                    k.copy(t, xT[:, dc, blk * 512:(blk + 1) * 512])
                else:
                    k.stt(t, xT[:, dc, blk * 512:(blk + 1) * 512], fgs[:, dc:dc + 1], rstd, ALU.mult, ALU.mult)
                tok = p.dma("sp", dst[:, dc, blk * 512:(blk + 1) * 512], t)
                p.out_tokens.append(tok)

    def peer(ci, l):
        g2 = mod_vec(ci, l, 5)
        skb = Buf(arC.ap[:, 0:2048].rearrange("p (h k) -> p h k", h=16), "skb")
        s2row = Buf(arC.ap[:, 2048:2304].bitcast(F32), "s2row")
        candh = Buf(arC.ap[:, 2304:2816].bitcast(F32), "candh")
        cand2h = Buf(arC.ap[:, 2816:3328].bitcast(F32), "cand2h")
        sv = Buf(arC.ap[:, 3328:3840].bitcast(F32).rearrange("p (h k) -> p h k", h=16), "sv")
        cv = Buf(arC.ap[:, 3840:4096].bitcast(F32).rearrange("p (h k) -> p h k", h=8), "cv")
        dtmp = Buf(arC.ap[:, 4096:4352].bitcast(F32).rearrange("p (h k) -> p h k", h=8), "dtmp")
        etmp = Buf(arC.ap[:, 4352:4416].bitcast(F32).rearrange("p (h k) -> p h k", h=8), "etmp")
        pcs = [skb, s2row, candh, cand2h, sv, cv, dtmp, etmp]
        for b_ in pcs:
            p.alias(b_, [arC])
        qT = p.sub(arB[:, 0:16384], "qT")
        s_sb = Buf(arB.ap[:, 16384:20480].bitcast(F32).rearrange("p (h k) -> p h k", h=16), "s_sb")
        p.alias(qT, [arB])
        p.alias(s_sb, [arB])
        qTv = qT.ap.rearrange("p (h t) -> p h t", h=16)
        p.dma("pool", skb, skT[l])
        for hb in range(4):
            w = wget()
            wv = w.ap.rearrange("p (c j) -> p c j", c=16)
            p.dma("pool", View(w, wv), w_q[l, :, hb * 512:(hb + 1) * 512].rearrange("(c p) j -> p c j", p=128))
            for hj in range(4):
                hc = hb * 4 + hj
                for blk in range(2):
                    ps = k.bank()
                    for dc in range(NCH):
                        k.mm(ps, View(w, wv[:, dc, hj * 128:(hj + 1) * 128]), hT[:, dc, blk * 512:(blk + 1) * 512],
                             start=(dc == 0), stop=(dc == NCH - 1))
                    k.copy(View(qT, qTv[:, hc, blk * 512:(blk + 1) * 512]), ps, eng="act")
        if flags.get("pphase", 9) < 2:
            p.alias(arB, [qT, s_sb])
            p.alias(arC, pcs)
            return
        NB3 = 3
        Cf = [Buf(arA.ap[:, i * 4096:(i + 1) * 4096].bitcast(F32), "Cf%d" % i) for i in range(NB3)]
        Eb = [p.sub(arA[:, 12288 + i * 2048:12288 + (i + 1) * 2048], "Eb%d" % i) for i in range(NB3)]
        Gc = [p.sub(arA[:, 18432 + i * 2048:18432 + (i + 1) * 2048], "Gc%d" % i) for i in range(2)]
        for b_ in Cf + Eb + Gc:
            p.alias(b_, wslot)
        for tt_ in range(flags.get("ntt", 8)):
            tsl = slice(tt_ * 128, (tt_ + 1) * 128)
            banks = [k.bank() for _ in range(4)]
            for hc in range(16):
                k.mm(View(banks[hc // 4], banks[hc // 4].ap[:, (hc % 4) * 128:(hc % 4 + 1) * 128]),
                     View(qT, qTv[:, hc, tsl]), skb[:, hc, :], start=True, stop=True)
            for b4 in range(4):
                k.copy(s_sb[:, b4 * 4:(b4 + 1) * 4, :],
                       View(banks[b4], banks[b4].ap.rearrange("p (a b) -> p a b", a=4)), eng="act")
            for hc in range(16):
                sa, s2a, sva = A(s_sb[:, hc, :]), s2row.ap, sv.ap
                p.op("dve", lambda e, sa=sa, o=sva[:, hc, 0:8]: e.max(out=o, in_=sa), reads=[s_sb], writes=[sv])
                p.op("dve", lambda e, sa=sa, s2a=s2a, o=sva[:, hc, 0:8]: e.match_replace(
                    out=s2a, in_to_replace=o, in_values=sa, imm_value=NEG), reads=[s_sb, sv], writes=[s2row])
                p.op("dve", lambda e, s2a=s2a, o=sva[:, hc, 8:16]: e.max(out=o, in_=s2a), reads=[s2row], writes=[sv])
            for h in range(8):
                c3 = candh.ap.rearrange("p (i j) -> p i j", i=16)
                k.tt(View(candh, c3), View(sv, bc(sv.ap[:, 2 * h, :], [128, 16, 16], 2)),
                     View(sv, bc(sv.ap[:, 2 * h + 1, :], [128, 16, 16], 1)), ALU.add)
                ca, c2a, cva = candh.ap, cand2h.ap, cv.ap
                p.op("dve", lambda e, ca=ca, o=cva[:, h, 0:8]: e.max(out=o, in_=ca), reads=[candh], writes=[cv])
                p.op("dve", lambda e, ca=ca, c2a=c2a, o=cva[:, h, 0:8]: e.match_replace(
                    out=c2a, in_to_replace=o, in_values=ca, imm_value=NEG), reads=[candh, cv], writes=[cand2h])
                p.op("dve", lambda e, c2a=c2a, o=cva[:, h, 8:16]: e.max(out=o, in_=c2a), reads=[cand2h], writes=[cv])
            k.tt(dtmp, cv, View(cv, bc(cv.ap[:, :, 0], [128, 8, 16], 2)), ALU.subtract)
            k.act(dtmp, dtmp, AF.Exp)
            da, ea = A(dtmp), A(etmp[:, :, 0])
            p.op("dve", lambda e, da=da, ea=ea: e.reduce_sum(out=ea, in_=da, axis=AX.X), reads=[dtmp], writes=[etmp])
            k.act(etmp[:, :, 1], etmp[:, :, 0], AF.Ln)
            k.stt(etmp[:, :, 2], etmp[:, :, 1], -1.0, cv[:, :, 0], ALU.mult, ALU.subtract)
            it = 0
            for ic in range(8):
                gcb = Gc[ic % 2]
                for h in range(8):
                    Cc, Ee = Cf[it % NB3], Eb[it % NB3]
                    it += 1
                    s1b = View(s_sb, bc(s_sb.ap[:, 2 * h, ic * 16:(ic + 1) * 16], [128, 16, 128], 2))
                    s2b = View(s_sb, bc(s_sb.ap[:, 2 * h + 1, :], [128, 16, 128], 1))
                    c3 = Cc.ap.rearrange("p (i j) -> p i j", i=16)
                    k.tt(View(Cc, c3), s1b, s2b, ALU.add, eng=("pool" if (it % 2 == 0 and flags.get("pooladd", False)) else "dve"))
                    k.act(Ee, Cc, AF.Exp, bias=etmp[:, h, 2:3])
                    k.stt(Ee, Cc, cv[:, h, 15:16], Ee, ALU.is_ge, ALU.mult)
                    for q4 in range(4):
                        k.mm(k.pb[q4], ident_bf, Ee[:, q4 * 512:(q4 + 1) * 512], h == 0, h == 7, inc=(q4 == 3))
                for q4 in range(4):
                    k.copy(gcb[:, q4 * 512:(q4 + 1) * 512], k.pb[q4], eng="act")
                p.dma("sp", View(Gd_b, Gd[tsl, ic * 2048:(ic + 1) * 2048]), gcb)
        for w_ in wslot:
            p.alias(w_, Cf + Eb + Gc)
        if flags.get("pphase", 9) < 3:
            p.alias(arB, [qT, s_sb])
            p.alias(arC, pcs)
            return
        Gt = [p.sub(arB[:, i * 4096:(i + 1) * 4096], "Gt%d" % i) for i in range(2)]
        WT = [p.sub(arB[:, 8192 + i * 4096:8192 + (i + 1) * 4096], "WT%d" % i) for i in range(2)]
        actb = [p.sub(arB[:, 16384 + i * 512:16384 + (i + 1) * 512], "actb%d" % i) for i in range(4)]
        for b_ in Gt + WT:
            p.alias(b_, [qT])
        for b_ in actb:
            p.alias(b_, [s_sb])
        ptrs = [p.sub(ptr[:, i * 512:(i + 1) * 512], "ptrs%d" % i) for i in range(2)]
        for b_ in ptrs:
            p.alias(b_, [ptr])
        for eb in range(flags.get("neb", 32)):
            e0 = eb * 512
            wu = wget()
            wuv = wu.ap.rearrange("p (c j) -> p c j", c=16)
            p.dma("pool", View(wu, wuv), uT[l, :, e0:e0 + 512].rearrange("(c p) j -> p c j", p=128))
            wv_ = wget()
            wvv = wv_.ap.rearrange("p (c j) -> p c j", c=4)
            p.dma("pool", View(wv_, wvv), pv[l, e0:e0 + 512, :].rearrange("(c p) j -> p c j", p=128))
            gt = Gt[eb % 2]
            gtv = gt.ap.rearrange("p (a j) -> p a j", a=8)
            p.dma("sp", View(gt, gtv), View(Gd_b, Gd[:, e0:e0 + 512].rearrange("(a p) j -> p a j", p=128)))
            wt = WT[eb % 2]
            wtv = wt.ap.rearrange("p (c t) -> p c t", c=4)
            def do_transposes(tt_):
                tsl = slice(tt_ * 128, (tt_ + 1) * 128)
                ab = actb[tt_ % 4]
                pt_ = ptrs[tt_ % 2]
                for c4 in range(4):
                    pa, ia, ida = pt_.ap[:, c4 * 128:(c4 + 1) * 128], A(ab[:, c4 * 128:(c4 + 1) * 128]), ident_bf.ap
                    p.op("pe", lambda e, pa=pa, ia=ia, ida=ida: e.transpose(pa, ia, ida),
                         reads=[ab, ident_bf], writes=[pt_], inc=(c4 == 3))
                k.copy(View(wt, wtv[:, :, tsl]), View(pt_, pt_.ap.rearrange("p (c t) -> p c t", c=4)), eng="act")

            for tt_ in range(8):
                tsl = slice(tt_ * 128, (tt_ + 1) * 128)
                ps = k.bank()
                for dc in range(NCH):
                    k.mm(ps, hT[:, dc, tsl], View(wu, wuv[:, dc, :]), start=(dc == 0), stop=(dc == NCH - 1))
                ab = actb[tt_ % 4]
                k.act(ab, ps, AF.Gelu_apprx_tanh)
                k.tt(ab, ab, View(gt, gtv[:, tt_, :]), ALU.mult)
                if tt_ >= 1:
                    do_transposes(tt_ - 1)
            do_transposes(7)
            for dc in range(NCH):
                for blk in range(2):
                    ps = k.bank()
                    for c4 in range(4):
                        k.mm(ps, View(wv_, wvv[:, c4, dc * 128:(dc + 1) * 128]),
                             View(wt, wtv[:, c4, blk * 512:(blk + 1) * 512]), start=(c4 == 0), stop=(c4 == 3))
                    xv = xT[:, dc, blk * 512:(blk + 1) * 512]
                    k.stt(xv, ps, g2[:, dc:dc + 1], xv, ALU.mult, ALU.add)
        p.alias(arB, Gt + WT + actb)
        p.alias(arC, pcs)
        p.alias(ptr, ptrs)


    SCALE = 1.0 / math.sqrt(128.0)

    def attention(ci, ps_):
        sample = (ps_ == 1)
        g1 = mod_vec(ci, 0, 2)
        Qb = p.sub(arB[:, 0:4096], "Qb")
        Ktb = p.sub(arB[:, 4096:6144], "Ktb")
        Vb = p.sub(arB[:, 6144:8192], "Vb")
        mixb = p.sub(arB[:, 8192:12288], "mixb")
        PT = [p.sub(arB[:, 12288 + i * 512:12800 + i * 512], "PT%d" % i) for i in range(2)]
        O1n = Buf(arB.ap[:, 13312:15360].bitcast(F32).rearrange("p (a t) -> p a t", a=2), "O1n")
        tmpO = Buf(arB.ap[:, 15360:16384].bitcast(F32), "tmpO")
        sqd = Buf(arB.ap[:, 16384:17408].rearrange("p (a t) -> p a t", a=2), "sqd")
        cK = p.sub(arB[:, 17408:18432], "cK")
        cV = p.sub(arB[:, 18432:19456], "cV")
        recs = Buf(arB.ap[:, 19456:20480].bitcast(F32), "recs")
        allb = [Qb, Ktb, Vb, mixb] + PT + [O1n, tmpO, sqd, cK, cV, recs]
        for b_ in allb:
            p.alias(b_, [arB])
        ropeT = Buf(arC.ap[:, 0:2048].rearrange("p (a t) -> p a t", a=2), "ropeT")
        masks = Buf(arC.ap[:, 2048:3072].rearrange("p (a t) -> p a t", a=2), "masks")
        xb = Buf(arC.ap[:, 3072:3584], "xb")
        rperm = Buf(arC.ap[:, 3584:3712], "rperm")
        acs = [ropeT, masks, xb, rperm]
        for b_ in acs:
            p.alias(b_, [arC])
        if sample:
            p.dma("pool", ropeT, rope_in.rearrange("a p t -> p a t"))
            p.dma("pool", rperm, rperm_in)
            p.dma("pool", masks, masks_in.rearrange("a p t -> p a t"))
        sbk = [k.pb[0], k.pb[1], k.pb[2]]
        obk = [k.pb[3], k.pb[4]]
        smk = k.pb[5]
        msk = k.pb[6]
        st_i = [0]

        def rope_tile(dst, ps, blk):
            k.copy(xb, ps, eng="act")
            k.mm(msk, rperm, xb, True, True)
            a0, a1, a2 = A(rt1), A(ps), A(ropeT[:, 0, blk * 512:(blk + 1) * 512])
            p.op("dve", lambda e, a0=a0, a1=a1, a2=a2: e.tensor_tensor(out=a0, in0=a1, in1=a2, op=ALU.mult),
                 reads=[ps, ropeT, msk], writes=[rt1])
            k.tt(rt2, msk, ropeT[:, 1, blk * 512:(blk + 1) * 512], ALU.mult)
            k.tt(dst, rt1, rt2, ALU.add)

        def proj_fm(dst, wcols, rope):
            for blk in range(2):
                ps = sbk[st_i[0] % 3]
                st_i[0] += 1
                for dc in range(NCH):
                    k.mm(ps, wcols[:, dc, :], hT[:, dc, blk * 512:(blk + 1) * 512], dc == 0, dc == NCH - 1)
                if rope:
                    rope_tile(dst[:, blk * 512:(blk + 1) * 512], ps, blk)
                else:
                    k.copy(dst[:, blk * 512:(blk + 1) * 512], ps, eng="act")

        def proj_tm(wcols, ncols, tt_):
            ps = sbk[st_i[0] % 3]
            st_i[0] += 1
            for dc in range(NCH):
                k.mm(ps[:, 0:ncols], hT[:, dc, tt_ * 128:(tt_ + 1) * 128], wcols[:, dc, :], dc == 0, dc == NCH - 1)
            return ps

        def core(qv, N, chunks, ndvt):
            nchk = len(chunks)
            for ci_, (kt, vs, mask) in enumerate(chunks):
                sp = sbk[st_i[0] % 3]
                st_i[0] += 1
                k.mm(sp[:, 0:N], kt, qv, True, mask is None)
                if mask is not None:
                    k.mm(sp[:, 0:N], ident_bf, mask[:, 0:N], False, True)
                pt = PT[ci_ % 2]
                k.act(pt[:, 0:N], sp[:, 0:N], AF.Exp, scale=SCALE)
                for dvt in range(ndvt):
                    k.mm(obk[dvt][:, 0:N], vs[dvt], pt[:, 0:N], ci_ == 0, ci_ == nchk - 1)
                k.mm(smk[:, 0:N], ones_bf, pt[:, 0:N], ci_ == 0, ci_ == nchk - 1)

        def wout_partial(rows0, nh):
            wo = wget()
            wov = wo.ap[:, 0:nh * 2048].rearrange("p (h j) -> p h j", h=nh)
            p.dma("pool", View(wo, wov), w_out[rows0:rows0 + nh * 128, :].rearrange("(h p) j -> p h j", p=128))
            mv = mixb.ap[:, 0:nh * 1024].rearrange("p (h t) -> p h t", h=nh)
            for dc in range(NCH):
                for blk in range(2):
                    ps = sbk[st_i[0] % 3]
                    st_i[0] += 1
                    for h in range(nh):
                        k.mm(ps, View(wo, wov[:, h, dc * 128:(dc + 1) * 128]),
                             View(mixb, mv[:, h, blk * 512:(blk + 1) * 512]), h == 0, h == nh - 1)
                    xv = xT[:, dc, blk * 512:(blk + 1) * 512]
                    k.stt(xv, ps, g1[:, dc:dc + 1], xv, ALU.mult, ALU.add)

        for g in flags.get('agroups', (0, 1)):
            wq = wget()
            wqv = wq.ap.rearrange("p (c j) -> p c j", c=16)
            p.dma("pool", View(wq, wqv), w_in[:, g * 512:(g + 1) * 512].rearrange("(c p) j -> p c j", p=128))
            wkv = wget()
            wkvv = wkv.ap[:, 0:4096].rearrange("p (c j) -> p c j", c=16)
            p.dma("pool", View(wkv, wkvv[:, :, 0:128]),
                  w_in[:, 1024 + g * 128:1152 + g * 128].rearrange("(c p) j -> p c j", p=128))
            p.dma("pool", View(wkv, wkvv[:, :, 128:256]),
                  w_in[:, 1280 + g * 128:1408 + g * 128].rearrange("(c p) j -> p c j", p=128))
            Qv = Qb.ap.rearrange("p (h t) -> p h t", h=4)
            Ktv = Ktb.ap[:, 0:1024]
            Vv = Vb.ap[:, 0:1024].rearrange("p (a d) -> p a d", a=8)
            mv = mixb.ap.rearrange("p (h t) -> p h t", h=4)
            for h in range(flags.get("dq", 4)):
                proj_fm(View(Qb, Qv[:, h, :]), View(wq, wqv[:, :, h * 128:(h + 1) * 128]), sample)
            if flags.get("dk", 1):
                proj_fm(View(Ktb, Ktv), View(wkv, wkvv[:, :, 0:128]), sample)
            for tt_ in range(flags.get("dtm", 8)):
                if sample:
                    ps = proj_tm(View(wkv, wkvv[:, :, 128:256]), 128, tt_)
                    k.copy(View(Vb, Vv[:, tt_, :]), ps[:, 0:128])
                else:
                    ps = proj_tm(View(wkv, wkvv[:, :, 0:256]), 256, tt_)
                    sg = stage[tt_ % 2]
                    if flags.get("x1", 1):
                        k.copy(sg, ps[:, 0:256], eng="act")
                    if flags.get("x2", 1):
                        k.copy(View(Vb, Vv[:, tt_, :]), sg[:, 128:256])
                    for (dst, c0) in ((na_k, 0), (na_v, 128)):
                        if not flags.get("x3", 1):
                            continue
                        tok = p.dma("sp", dst[tt_ * 128:(tt_ + 1) * 128, g * 128:(g + 1) * 128], sg[:, c0:c0 + 128])
                        p.out_tokens.append(tok)
            if sample:
                cKv = cK.ap[:, 0:512]
                cVv = cV.ap[:, 0:512].rearrange("p (a d) -> p a d", a=4)
                p.dma("pool", View(cK, cKv), caK_in[:, g, :])
                p.dma("pool", View(cV, cVv), caV_in[:, :, g, :])
            nqb = flags.get('nqb', 8)
            for qb in range(nqb):
                t0 = qb * 128
                qv = View(Qb, Qv[:, :, t0:t0 + 128])
                chunks = []
                if sample:
                    for kc in range(4):
                        chunks.append((View(cK, cKv[:, kc * 128:(kc + 1) * 128]), [View(cV, cVv[:, kc, :])], None))
                    for nb in (qb - 1, qb, qb + 1):
                        if 0 <= nb < 8:
                            m_ = None if nb == qb else (masks[:, 0, :] if nb == qb - 1 else masks[:, 1, :])
                            chunks.append((View(Ktb, Ktv[:, nb * 128:(nb + 1) * 128]), [View(Vb, Vv[:, nb, :])], m_))
                else:
                    s_ = qb // 2
                    for kc in range(2):
                        tk = s_ * 2 + kc
                        chunks.append((View(Ktb, Ktv[:, tk * 128:(tk + 1) * 128]), [View(Vb, Vv[:, tk, :])], None))
                core(qv, 512, chunks, 1)
                for h in range(4):
                    k.ts(smk[:, h * 128:(h + 1) * 128], smk[:, h * 128:(h + 1) * 128],
                         esink[:, g * 4 + h:g * 4 + h + 1], ALU.add)
                ra, sa_ = recs.ap, smk.ap
                p.op("dve", lambda e, ra=ra, sa_=sa_: e.reciprocal(out=ra, in_=sa_), reads=[smk], writes=[recs])
                k.tt(View(mixb, mv[:, :, t0:t0 + 128]), View(obk[0], obk[0].ap.rearrange("p (h t) -> p h t", h=4)),
                     View(recs, recs.ap.rearrange("p (h t) -> p h t", h=4)), ALU.mult)
            if flags.get('awout', True):
                wout_partial(g * 512, 4)

        for h in flags.get('bheads', (0, 1, 2, 3)):
            wqk = wget()
            wqkv = wqk.ap.rearrange("p (c j) -> p c j", c=16)
            p.dma("pool", View(wqk, wqkv[:, :, 0:256]),
                  w_in[:, 1536 + h * 256:1792 + h * 256].rearrange("(c p) j -> p c j", p=128))
            p.dma("pool", View(wqk, wqkv[:, :, 256:512]),
                  w_in[:, 2560 + h * 256:2816 + h * 256].rearrange("(c p) j -> p c j", p=128))
            wv_ = wget()
            wvv = wv_.ap[:, 0:4096].rearrange("p (c j) -> p c j", c=16)
            p.dma("pool", View(wv_, wvv), w_in[:, 3584 + h * 256:3840 + h * 256].rearrange("(c p) j -> p c j", p=128))
            Qv = Qb.ap[:, 0:2048].rearrange("p (j t) -> p j t", j=2)
            Ktv = Ktb.ap.rearrange("p (j t) -> p j t", j=2)
            Vv = Vb.ap.rearrange("p (a d) -> p a d", a=8)
            mv = mixb.ap[:, 0:2048].rearrange("p (a t) -> p a t", a=2)
            for j in range(2):
                proj_fm(View(Qb, Qv[:, j, :]), View(wqk, wqkv[:, :, j * 128:(j + 1) * 128]), sample)
                proj_fm(View(Ktb, Ktv[:, j, :]), View(wqk, wqkv[:, :, 256 + j * 128:384 + j * 128]), sample)
            for tt_ in range(8):
                ps = proj_tm(View(wv_, wvv), 256, tt_)
                if sample:
                    k.copy(View(Vb, Vv[:, tt_, :]), ps[:, 0:256])
                if not sample:
                    sg = stage[0]
                    k.copy(sg, ps[:, 0:256], eng="act")
                    k.copy(View(Vb, Vv[:, tt_, :]), sg)
                    tok = p.dma("sp", nb_v[tt_ * 128:(tt_ + 1) * 128, h * 256:(h + 1) * 256], sg)
                    p.out_tokens.append(tok)
                    ps2 = proj_tm(View(wqk, wqkv[:, :, 256:512]), 256, tt_)
                    sg = stage[1]
                    k.copy(sg, ps2[:, 0:256], eng="act")
                    tok = p.dma("sp", nb_k[tt_ * 128:(tt_ + 1) * 128, h * 256:(h + 1) * 256], sg)
                    p.out_tokens.append(tok)
            if sample:
                cKv = cK.ap.rearrange("p (j t) -> p j t", j=2)
                cVv = cV.ap.rearrange("p (a d) -> p a d", a=4)
                p.dma("pool", View(cK, cKv), cbK_in[:, h, :, :])
                p.dma("pool", View(cV, cVv), cbV_in[:, :, h, :])
            nq = 2 if sample else 4
            nq = min(nq, flags.get('nqB', 9))
            N = 512 if sample else 256
            for qi in range(nq):
                tq = slice(qi * N, (qi + 1) * N)
                for j in range(2):
                    chunks = []
                    if sample:
                        for kc in range(4):
                            chunks.append((View(cK, cKv[:, j, kc * 128:(kc + 1) * 128]),
                                           [View(cV, cVv[:, kc, d_ * 128:(d_ + 1) * 128]) for d_ in range(2)], None))
                        for kc in range(8):
                            chunks.append((View(Ktb, Ktv[:, j, kc * 128:(kc + 1) * 128]),
                                           [View(Vb, Vv[:, kc, d_ * 128:(d_ + 1) * 128]) for d_ in range(2)], None))
                    else:
                        for kc in range(2):
                            tk = qi * 2 + kc
                            chunks.append((View(Ktb, Ktv[:, j, tk * 128:(tk + 1) * 128]),
                                           [View(Vb, Vv[:, tk, d_ * 128:(d_ + 1) * 128]) for d_ in range(2)], None))
                    core(View(Qb, Qv[:, j, tq]), N, chunks, 2)
                    ra, sa_ = recs.ap[:, 0:N], smk.ap[:, 0:N]
                    p.op("dve", lambda e, ra=ra, sa_=sa_: e.reciprocal(out=ra, in_=sa_), reads=[smk], writes=[recs])
                    for d_ in range(2):
                        if j == 0:
                            k.tt(O1n[:, d_, 0:N], obk[d_][:, 0:N], recs[:, 0:N], ALU.mult)
                        else:
                            k.stt(tmpO[:, 0:N], obk[d_][:, 0:N], lamt[:, 3:4], recs[:, 0:N], ALU.mult, ALU.mult)
                            k.tt(O1n[:, d_, 0:N], O1n[:, d_, 0:N], tmpO[:, 0:N], ALU.add)
                for d_ in range(2):
                    k.act(sqd[:, d_, 0:N], O1n[:, d_, 0:N], AF.Square)
                for d_ in range(2):
                    k.mm(msk[:, 0:N], ones_bf, sqd[:, d_, 0:N], d_ == 0, d_ == 1)
                k.act(recs[:, 0:N], msk[:, 0:N], AF.Sqrt, bias=epsb, scale=1.0 / 256.0)
                ra = recs.ap[:, 0:N]
                p.op("dve", lambda e, ra=ra: e.reciprocal(out=ra, in_=ra), reads=[recs], writes=[recs])
                for d_ in range(2):
                    k.stt(View(mixb, mv[:, d_, tq]), O1n[:, d_, 0:N], subgl[:, d_:d_ + 1], recs[:, 0:N], ALU.mult, ALU.mult)
            if flags.get('awout', True):
                wout_partial(1024 + h * 256, 2)
        p.alias(arB, allb)
        p.alias(arC, acs)


    def ssd(ci, ps_):
        sample = (ps_ == 1)
        g1 = mod_vec(ci, 1, 2)
        seqs = [(0, 8)] if sample else [(0, 2), (2, 4), (4, 6), (6, 8)]
        nseq = len(seqs)
        slen = NT // nseq
        szT = Buf(arB.ap[:, 0:4096].rearrange("p (a t) -> p a t", a=4), "szT")
        xcT = Buf(arB.ap[:, 4096:8192].rearrange("p (a t) -> p a t", a=4), "xcT")
        xtok = Buf(arB.ap[:, 8192:12288].rearrange("p (c j) -> p c j", c=8), "xtok")
        hbin = Buf(arB.ap[:, 12288:16384].rearrange("p (c j) -> p c j", c=8), "hbin")
        BT = Buf(arB.ap[:, 16384:17408], "BT")
        CT = Buf(arB.ap[:, 17408:18432], "CT")
        Btok = Buf(arB.ap[:, 18432:19456].rearrange("p (c j) -> p c j", c=8), "Btok")
        hbf = [Buf(arB.ap[:, 19456 + i * 256:19712 + i * 256], "hbf%d" % i) for i in range(2)]
        bB = [szT, xcT, xtok, hbin, BT, CT, Btok] + hbf
        for b_ in bB:
            p.alias(b_, [arB])
        wdt = Buf(arC.ap[:, 0:2048].rearrange("p (c j) -> p c j", c=16), "wdt")
        dtall = Buf(arC.ap[:, 2048:4096].bitcast(F32).rearrange("p (c j) -> p c j", c=8), "dtall")
        Tm = Buf(arC.ap[:, 4096:4608].bitcast(F32).rearrange("p (a t) -> p a t", a=2), "Tm")
        bC = [wdt, dtall, Tm]
        for b_ in bC:
            p.alias(b_, [arC])
        S2 = wslot[2]
        xpre = Buf(S2.ap[:, 0:2048].bitcast(F32), "xpre")
        acc = Buf(S2.ap[:, 2048:4096].bitcast(F32), "acc")
        aT = Buf(S2.ap[:, 0:1024].bitcast(F32).rearrange("p (j t) -> p j t", j=4), "aT")
        LT = Buf(S2.ap[:, 1024:2048].bitcast(F32).rearrange("p (j t) -> p j t", j=4), "LT")
        EB = Buf(S2.ap[:, 2048:3072].bitcast(F32).rearrange("p (j t) -> p j t", j=4), "EB")
        MTs = [Buf(S2.ap[:, 3072 + i * 512:3584 + i * 512].rearrange("p (j t) -> p j t", j=4), "MT%d" % i) for i in range(2)]
        CeTs = [Buf(S2.ap[:, 4096 + i * 512:4608 + i * 512].rearrange("p (j t) -> p j t", j=4), "CeT%d" % i) for i in range(2)]
        hst = [Buf(S2.ap[:, 5120 + i * 512:5632 + i * 512].bitcast(F32), "hst%d" % i) for i in range(2)]
        y2 = Buf(S2.ap[:, 6144:6656].bitcast(F32), "y2")
        Xs = Buf(S2.ap[:, 6656:7680].rearrange("p (a t) -> p a t", a=4), "Xs")
        sm = Buf(S2.ap[:, 7680:7808].bitcast(F32).rearrange("p (a t) -> p a t", a=8), "sm")
        convb = [xpre, acc]
        sweepb = [aT, LT, EB] + MTs + CeTs + hst + [y2, Xs, sm]
        for b_ in convb + sweepb:
            p.alias(b_, [S2])
        k.ws2 = 0

        def wget2():
            b = wslot[k.ws2 % 2]
            k.ws2 += 1
            return b

        sbk = [k.pb[0], k.pb[1], k.pb[2]]
        st_i = [0]

        def nb():
            b = sbk[st_i[0] % 3]
            st_i[0] += 1
            return b

        p.dma("pool", wdt, sw_in[:, 10240:10368].rearrange("(c p) j -> p c j", p=128))
        p.dma("sp", Tm, tri_in[0:2].rearrange("a p t -> p a t"))
        for tt_ in range(8):
            ps = nb()
            for dc in range(NCH):
                k.mm(ps[:, 0:128], hT[:, dc, tt_ * 128:(tt_ + 1) * 128], wdt[:, dc, :], dc == 0, dc == NCH - 1)
            k.tt(dtall[:, tt_, :], ps[:, 0:128], dtb, ALU.add)
        k.act(dtall, dtall, AF.Exp)
        k.act(dtall, dtall, AF.Ln, bias=1.0)

        def conv_silu(dst, tile_idx):
            k.ts(acc, xpre, convw[:, tile_idx, 2:3], ALU.mult, convw[:, tile_idx, 5:6], ALU.add)
            a3 = acc.ap.rearrange("p (s t) -> p s t", s=nseq)
            x3 = xpre.ap.rearrange("p (s t) -> p s t", s=nseq)
            for kk_ in (0, 1, 3, 4):
                sh = kk_ - 2
                o_lo, o_hi = max(0, -sh), slen - max(0, sh)
                i_lo, i_hi = max(0, sh), slen - max(0, -sh)
                k.stt(View(acc, a3[:, :, o_lo:o_hi]), View(xpre, x3[:, :, i_lo:i_hi]), convw[:, tile_idx, kk_:kk_ + 1],
                      View(acc, a3[:, :, o_lo:o_hi]), ALU.mult, ALU.add)
            k.act(dst, acc, AF.Silu)

        def proj_to_xpre(wcols):
            for blk in range(2):
                ps = nb()
                for dc in range(NCH):
                    k.mm(ps, wcols[:, dc, :], hT[:, dc, blk * 512:(blk + 1) * 512], dc == 0, dc == NCH - 1)
                k.copy(xpre[:, blk * 512:(blk + 1) * 512], ps, eng="act")

        smg = Buf(arC.ap[:, 0:1536].bitcast(F32).rearrange("p (r c j) -> p r c j", r=6, c=8), "smg")
        p.alias(smg, [wdt])
        bC.append(smg)

        def group_small(g):
            for d_ in range(2):
                cs = slice(d_ * 64 + g * 8, d_ * 64 + g * 8 + 8)
                k.copy(smg[:, 5, :, d_ * 8:(d_ + 1) * 8], dtall[:, :, cs])
                k.tt(smg[:, 0, :, d_ * 8:(d_ + 1) * 8], dtall[:, :, cs],
                     View(ABt, bc(ABt.ap[:, cs], [128, 8, 8], 1)), ALU.mult)
            psm = k.pb[5]
            pv3 = psm.ap[:, 0:256].rearrange("p (c j) -> p c j", c=8)
            for c in range(8):
                for d_ in range(2):
                    k.mm(View(psm, pv3[:, c, d_ * 8:(d_ + 1) * 8]), Tm[:, d_, :], smg[:, 0, c, d_ * 8:(d_ + 1) * 8], True, True)
                k.mm(View(psm, pv3[:, c, 16:32]), onesF, smg[:, 0, c, :], True, True)
            k.copy(smg[:, 1], View(psm, pv3[:, :, 0:16]), eng="act")
            k.copy(smg[:, 3], View(psm, pv3[:, :, 16:32]), eng="act")
            k.ts(smg[:, 1], smg[:, 1], -1.0, ALU.mult)
            k.tt(smg[:, 2], smg[:, 3], smg[:, 1], ALU.add)
            k.act(smg[:, 2], smg[:, 2], AF.Exp)
            k.act(smg[:, 3], smg[:, 3], AF.Exp)
            k.tt(smg[:, 4], smg[:, 5], smg[:, 2], ALU.mult)

        for g in flags.get("sgroups", range(8)):
            group_small(g)
            for b_ in convb:
                p.alias(b_, sweepb)
            wz = wget2()
            wzv = wz.ap.rearrange("p (c j) -> p c j", c=16)
            p.dma("pool", View(wz, wzv), sw_in[:, g * 512:(g + 1) * 512].rearrange("(c p) j -> p c j", p=128))
            for t4 in range(4):
                for blk in range(2):
                    ps = nb()
                    for dc in range(NCH):
                        k.mm(ps, View(wz, wzv[:, dc, t4 * 128:(t4 + 1) * 128]), hT[:, dc, blk * 512:(blk + 1) * 512],
                             dc == 0, dc == NCH - 1)
                    k.act(szT[:, t4, blk * 512:(blk + 1) * 512], ps, AF.Silu)
            wx = wget2()
            wxv = wx.ap.rearrange("p (c j) -> p c j", c=16)
            p.dma("pool", View(wx, wxv), sw_in[:, 4096 + g * 512:4608 + g * 512].rearrange("(c p) j -> p c j", p=128))
            for t4 in range(4):
                proj_to_xpre(View(wx, wxv[:, :, t4 * 128:(t4 + 1) * 128]))
                conv_silu(xcT[:, t4, :], g * 4 + t4)
            wbc = wget2()
            wbcv = wbc.ap[:, 0:4096].rearrange("p (c j) -> p c j", c=16)
            p.dma("pool", View(wbc, wbcv[:, :, 0:128]),
                  sw_in[:, 8192 + g * 128:8320 + g * 128].rearrange("(c p) j -> p c j", p=128))
            p.dma("pool", View(wbc, wbcv[:, :, 128:256]),
                  sw_in[:, 9216 + g * 128:9344 + g * 128].rearrange("(c p) j -> p c j", p=128))
            proj_to_xpre(View(wbc, wbcv[:, :, 0:128]))
            conv_silu(BT, 32 + g)
            proj_to_xpre(View(wbc, wbcv[:, :, 128:256]))
            conv_silu(CT, 40 + g)
            sstage = flags.get("sstage", 9)
            for c in range(8 if sstage >= 2 else 0):
                csl = slice(c * 128, (c + 1) * 128)
                for t4 in range(4):
                    pa, ia, ida = ptr.ap[:, t4 * 128:(t4 + 1) * 128], A(xcT[:, t4, csl]), ident_bf.ap
                    p.op("pe", lambda e, pa=pa, ia=ia, ida=ida: e.transpose(pa, ia, ida),
                         reads=[xcT, ident_bf], writes=[ptr], inc=False)
                pa, ia, ida = ptr.ap[:, 512:640], A(BT[:, csl]), ident_bf.ap
                p.op("pe", lambda e, pa=pa, ia=ia, ida=ida: e.transpose(pa, ia, ida),
                     reads=[BT, ident_bf], writes=[ptr], inc=True)
                k.copy(xtok[:, c, :], ptr[:, 0:512], eng="act")
                k.copy(Btok[:, c, :], ptr[:, 512:640], eng="act")
            for b_ in sweepb:
                p.alias(b_, convb)

            for hh in range(2 if sstage >= 3 else 0):
                hs0 = g * 8 + hh * 4
                cols = [slice(hs0, hs0 + 4), slice(64 + hs0, 64 + hs0 + 4)]
                xsl = slice(hh * 256, (hh + 1) * 256)

                def chunk_small(c):
                    for d_ in range(2):
                        k.tt(sm[:, 0, d_ * 4:(d_ + 1) * 4], dtall[:, c, cols[d_]], ABt[:, cols[d_]], ALU.mult)
                        k.copy(sm[:, 5, d_ * 4:(d_ + 1) * 4], dtall[:, c, cols[d_]])
                    psm = k.pb[5]
                    for d_ in range(2):
                        k.mm(psm[:, d_ * 4:(d_ + 1) * 4], Tm[:, d_, :], sm[:, 0, d_ * 4:(d_ + 1) * 4], True, True)
                    k.mm(psm[:, 8:16], onesF, sm[:, 0, :], True, True)
                    k.copy(sm[:, 6:8, :], View(psm, psm.ap[:, 0:16].rearrange("p (a t) -> p a t", a=2)), eng="act")
                    k.ts(sm[:, 1, :], sm[:, 6, :], -1.0, ALU.mult)
                    k.tt(sm[:, 2, :], sm[:, 7, :], sm[:, 1, :], ALU.add)
                    k.act(sm[:, 2, :], sm[:, 2, :], AF.Exp)
                    k.act(sm[:, 3, :], sm[:, 7, :], AF.Exp)
                    k.tt(sm[:, 4, :], sm[:, 5, :], sm[:, 2, :], ALU.mult)

                def smc(row, c, d_):
                    return smg.ap[:, row, c, d_ * 8 + hh * 4:d_ * 8 + hh * 4 + 4]

                def scaled_x(dst, c, row, d_):
                    xv = xtok.ap[:, c, xsl].rearrange("p (j q) -> p j q", j=4)
                    k.tt(View(Xs, dst.rearrange("p (j q) -> p j q", j=4)), View(xtok, xv),
                         View(smg, bc(smc(row, c, d_), [128, 4, 64], 2)), ALU.mult)

                def state_update(d_, c, xw_ap):
                    stp = k.pb[6]
                    k.mm(stp[:, 0:256], Btok[:, c, :], View(Xs, xw_ap), True, True)
                    h3 = hst[d_].ap.rearrange("p (j q) -> p j q", j=4)
                    k.tt(View(hst[d_], h3), View(hst[d_], h3),
                         View(smg, bc(smc(3, c, d_), [128, 4, 64], 2)), ALU.mult)
                    k.tt(hst[d_], hst[d_], stp[:, 0:256], ALU.add)

                def init_state(d_, si):
                    if sample:
                        p.dma("sp", View(hst[d_], hst[d_].ap.rearrange("p (j q) -> p j q", j=4)), st_in[d_][:, hs0:hs0 + 4, :])
                    else:
                        k.memset(hst[d_], 0.0)

                def out_state(d_, si):
                    if sample:
                        return
                    ptf = k.pb[6]
                    for t2 in range(2):
                        pa, ia, ida = ptf.ap[:, 256 + t2 * 128:384 + t2 * 128], A(hst[d_][:, t2 * 128:(t2 + 1) * 128]), identF.ap
                        p.op("pe", lambda e, pa=pa, ia=ia, ida=ida: e.transpose(pa, ia, ida),
                             reads=[hst[d_], identF], writes=[ptf], inc=(t2 == 1))
                    k.copy(y2, ptf[:, 256:512], eng="act")
                    for t2 in range(2):
                        r0 = (si * 64 + hs0 + t2 * 2) * 64
                        tok = p.dma("sp", ns_out[d_][r0:r0 + 128, :], y2[:, t2 * 128:(t2 + 1) * 128])
                        p.out_tokens.append(tok)

                for si, (c0, c1) in enumerate(seqs):
                    init_state(1, si)
                    for c in reversed(range(c0, c1)):
                        k.copy(hbin[:, c, xsl], hst[1], eng="act")
                        scaled_x(Xs.ap[:, 3, :], c, 4, 1)
                        state_update(1, c, Xs.ap[:, 3, :])
                    out_state(1, si)
                for si, (c0, c1) in enumerate(seqs if sstage >= 4 else []):
                    init_state(0, si)
                    for c in range(c0, c1):
                        csl = slice(c * 128, (c + 1) * 128)
                        k.copy(hbf[0], hst[0], eng="act")
                        gp = k.pb[5]
                        k.mm(gp[:, 128:256], BT[:, csl], CT[:, csl], True, True)
                        k.copy(GTs, gp[:, 128:256], eng="act")
                        for d_ in range(2):
                            Dp, Ep = k.pb[2 * d_], k.pb[2 * d_ + 1]
                            k.tt(aT, View(Tm, bc(Tm.ap[:, d_, :], [128, 4, 128], 1)),
                                 View(smg, bc(smc(0, c, d_), [128, 4, 128], 2)), ALU.mult)
                            k.mm(Ep, onesF, View(aT, aT.ap.rearrange("p j t -> p (j t)")), True, True)
                            Ep3 = View(Ep, Ep.ap.rearrange("p (j t) -> p j t", j=4))
                            k.act(EB, Ep3, AF.Exp)
                            la, ea, ma = LT.ap, Ep3.ap, bc(maskn.ap[:, d_, :], [128, 4, 128], 1)
                            p.op("dve", lambda e, la=la, ea=ea, ma=ma: e.tensor_tensor(out=la, in0=ea, in1=ma, op=ALU.add),
                                 reads=[Ep, maskn, EB], writes=[LT])
                            for j in range(4):
                                k.act(LT[:, j, :], LT[:, j, :], AF.Exp,
                                      bias=View(smg, smg.ap[:, 1, c, d_ * 8 + hh * 4 + j:d_ * 8 + hh * 4 + j + 1]))
                            k.tt(MTs[d_], LT, View(GTs, bc(GTs.ap, [128, 4, 128], 1)), ALU.mult)
                            k.tt(CeTs[d_], EB, View(CT, bc(CT.ap[:, csl], [128, 4, 128], 1)), ALU.mult)
                            scaled_x(Xs.ap[:, d_, :], c, 5, d_)
                        yp = k.pb[4]
                        for j in range(4):
                            q = slice(j * 64, (j + 1) * 64)
                            k.mm(yp[:, q], MTs[0][:, j, :], Xs[:, 0, q], True, False)
                            k.mm(yp[:, q], CeTs[0][:, j, :], hbf[0][:, q], False, False)
                            k.mm(yp[:, q], MTs[1][:, j, :], Xs[:, 1, q], False, False)
                            k.mm(yp[:, q], CeTs[1][:, j, :], hbin[:, c, hh * 256 + j * 64:hh * 256 + (j + 1) * 64], False, True)
                        xv = xtok.ap[:, c, xsl].rearrange("p (j q) -> p j q", j=4)
                        k.tt(View(y2, y2.ap.rearrange("p (j q) -> p j q", j=4)), View(xtok, xv),
                             View(dsk, bc(dsk.ap[:, hs0:hs0 + 4], [128, 4, 64], 2)), ALU.mult)
                        k.tt(y2, y2, yp[:, 0:256], ALU.add)
                        ptf = k.pb[6]
                        for t2 in range(2):
                            pa, ia, ida = ptf.ap[:, 256 + t2 * 128:384 + t2 * 128], A(y2[:, t2 * 128:(t2 + 1) * 128]), identF.ap
                            p.op("pe", lambda e, pa=pa, ia=ia, ida=ida: e.transpose(pa, ia, ida),
                                 reads=[y2, identF], writes=[ptf], inc=(t2 == 1))
                        k.tt(xcT[:, hh * 2:hh * 2 + 2, csl], View(ptf, ptf.ap[:, 256:512].rearrange("p (a t) -> p a t", a=2)),
                             szT[:, hh * 2:hh * 2 + 2, csl], ALU.mult)
                        scaled_x(Xs.ap[:, 2, :], c, 4, 0)
                        state_update(0, c, Xs.ap[:, 2, :])
                    out_state(0, si)

            sqv = hbin.ap.rearrange("p c j -> p (c j)")[:, 0:2048].rearrange("p (a t) -> p a t", a=4)
            if sstage < 5:
                continue
            for blk in range(2):
                bsl = slice(blk * 512, (blk + 1) * 512)
                k.act(View(hbin, sqv), xcT[:, :, bsl], AF.Square)
                ps = nb()
                for t4 in range(4):
                    k.mm(ps, ones_bf, View(hbin, sqv[:, t4, :]), t4 == 0, t4 == 3)
                k.act(rstd, ps, AF.Sqrt, bias=epsb, scale=1.0 / 512.0)
                rs = A(rstd)
                p.op("dve", lambda e, rs=rs: e.reciprocal(out=rs, in_=rs), reads=[rstd], writes=[rstd])
                for t4 in range(4):
                    k.stt(xcT[:, t4, bsl], xcT[:, t4, bsl], ngrp[:, g * 4 + t4:g * 4 + t4 + 1], rstd, ALU.mult, ALU.mult)
            wo = wget2()
            wov = wo.ap.rearrange("p (h j) -> p h j", h=4)
            p.dma("pool", View(wo, wov), sw_out[g * 512:(g + 1) * 512, :].rearrange("(h p) j -> p h j", p=128))
            for dc in range(NCH):
                for blk in range(2):
                    ps = nb()
                    for t4 in range(4):
                        k.mm(ps, View(wo, wov[:, t4, dc * 128:(dc + 1) * 128]), xcT[:, t4, blk * 512:(blk + 1) * 512],
                             t4 == 0, t4 == 3)
                    xv_ = xT[:, dc, blk * 512:(blk + 1) * 512]
                    k.stt(xv_, ps, g1[:, dc:dc + 1], xv_, ALU.mult, ALU.add)
        p.alias(arB, bB)
        p.alias(arC, bC)
        p.alias(S2, convb + sweepb)

    for ps_ in range(2):
        if ps_ not in flags.get("passes", (0, 1)):
            continue
        ci = ps_
        for dc in range(NCH):
            p.dma("sp", xT[:, dc, :], x_in[ps_][:, dc, :])
        for l in flags.get("layers", (0, 1)):
            if l == 0 and has_attn:
                adaln(ci, 0, 0)
                attention(ci, ps_)
            if l == 1 and has_ssd:
                adaln(ci, 1, 0)
                ssd(ci, ps_)
            if flags.get("peer", True):
                if flags.get("dbg", 9) >= 1:
                    adaln(ci, l, 1)
                if flags.get("pphase", 9) > 0:
                    peer(ci, l)
        final_norm(y_out[ps_])
    p.emit()
    return nc


def fm(x2d):
    T, Dd = x2d.shape
    return np.ascontiguousarray(x2d.T.reshape(Dd // 128, 128, T).transpose(1, 0, 2))


def unfm(a):
    P, C, T = a.shape
    return np.ascontiguousarray(a.transpose(1, 0, 2).reshape(C * P, T).T)


def vec_pc(v):
    return np.ascontiguousarray(v.reshape(-1, 128).T)


_CACHE = {}


def prepare_inputs(inp, core):
    f = np.float32
    b = core // 2
    m = {}
    xp = inp["x_prompt"][4 * core:4 * core + 4].reshape(NT, D)
    m["xp"] = fm(xp)
    m["xs"] = fm(inp["x_sample"][b])
    m["cond"] = np.ascontiguousarray(np.stack([vec_pc(inp["c_ctx"]), vec_pc(inp["c"][b])], axis=-1)).astype(f)
    m["w_mod"] = inp["w_mod"]
    m["b_mod"] = np.ascontiguousarray(np.stack([vec_pc(inp["b_mod"][l]) for l in range(2)], axis=1)).astype(f)
    m["ng"] = np.ascontiguousarray(np.stack([vec_pc(inp["norm_g"][l, kk]) for l in range(2) for kk in range(2)], axis=1))
    m["fg"] = vec_pc(inp["final_g"])
    m["w_q"] = inp["peer_w_q"]
    m["skT"] = np.ascontiguousarray(inp["peer_sub_keys"].transpose(0, 4, 1, 2, 3).reshape(2, 128, 16, 128))
    m["uT"] = _CACHE["uT"]
    m["pv"] = inp["peer_v"]
    m["ident"] = np.eye(128, dtype=f)
    m["attn_w_in"] = inp["attn_w_in"][0]
    m["attn_w_out"] = inp["attn_w_out"][0]
    m["esink"] = np.ascontiguousarray(np.broadcast_to(inp["attn_sink"][0].reshape(1, 8), (128, 8))).astype(f)
    m["lamv"] = np.ascontiguousarray(np.broadcast_to(inp["diff_lam"][0][None], (128, 4, 128))).astype(f)
    m["subg"] = vec_pc(inp["diff_subln_g"][0])
    m["rope"] = _CACHE["rope"]
    m["rperm"] = _CACHE["rperm"]
    m["masks"] = _CACHE["masks"]
    m["caK"] = np.ascontiguousarray(inp["cache_a_k"][b, 0].transpose(2, 1, 0))
    m["caV"] = np.ascontiguousarray(inp["cache_a_v"][b, 0].reshape(4, 128, 2, 128).transpose(1, 0, 2, 3))
    m["cbK"] = np.ascontiguousarray(inp["cache_b_k"][b, 0].transpose(3, 1, 2, 0))
    m["cbV"] = np.ascontiguousarray(inp["cache_b_v"][b, 0].reshape(4, 128, 4, 256).transpose(1, 0, 2, 3))
    m["ssd_w_in"] = inp["ssd_w_in"][0]
    m["ssd_w_out"] = inp["ssd_w_out"][0]
    cw = inp["ssd_conv_w"][0]
    cb = inp["ssd_conv_b"][0]
    cwb = np.concatenate([cw, cb[None]], axis=0)
    m["convw"] = np.ascontiguousarray(cwb.reshape(6, 48, 128).transpose(2, 1, 0)).astype(f)
    m["dtbias"] = np.ascontiguousarray(np.broadcast_to(inp["ssd_dt_bias"][0].reshape(1, 128), (128, 128))).astype(f)
    m["alog"] = np.ascontiguousarray(np.broadcast_to(inp["ssd_a_log"][0].reshape(1, 128), (128, 128))).astype(f)
    m["dskip"] = np.ascontiguousarray(np.broadcast_to(inp["ssd_d"][0].reshape(1, 64), (128, 64))).astype(f)
    m["ngrp"] = vec_pc(inp["ssd_norm_g"][0])
    m["tri"] = _CACHE["tri"]
    m["st_f"] = np.ascontiguousarray(inp["state_ssd_fwd"][b, 0].transpose(2, 0, 1))
    m["st_b"] = np.ascontiguousarray(inp["state_ssd_bwd"][b, 0].transpose(2, 0, 1))
    return m


def _const_tables():
    f = np.float32
    L = 1024
    t = np.arange(L)
    row = (t // 64).astype(f)
    col = (t % 64).astype(f)
    inv = (10000.0 ** (-np.arange(0, 64, 2, dtype=f) / 64.0)).astype(f)
    ar = row[:, None] * inv
    ac = col[:, None] * inv
    ang = np.concatenate([ar, ar, ac, ac], axis=-1)
    cos = np.cos(ang).astype(f).T
    sin = np.sin(ang).astype(f).T
    sign = np.ones((128, 1), f)
    sign[0:32] = -1.0
    sign[64:96] = -1.0
    _CACHE["rope"] = np.ascontiguousarray(np.stack([cos, sin * sign], axis=0))
    perm = np.arange(128)
    perm = np.where((perm % 64) < 32, perm + 32, perm - 32)
    R = np.zeros((128, 128), f)
    R[perm, np.arange(128)] = 1.0
    _CACHE["rperm"] = R
    kk = np.arange(128)[:, None]
    qq = np.arange(128)[None, :]
    mp = np.where(kk >= qq, 0.0, -30000.0).astype(f)
    mn = np.where(kk <= qq, 0.0, -30000.0).astype(f)
    tf = (kk <= qq).astype(f)
    tb = (kk >= qq).astype(f)
    _CACHE["tri"] = np.ascontiguousarray(np.stack([tf, tb, (tf - 1.0) * 30000.0, (tb - 1.0) * 30000.0], axis=0))
    _CACHE["masks"] = np.ascontiguousarray(np.stack([np.tile(mp, (1, 4)), np.tile(mn, (1, 4))], axis=0))


def kernel(**inp):
    inp = {k_: np.asarray(v) for k_, v in inp.items()}
    flags = _CACHE.get("flags", {})
    nc = build_program(flags)
    _CACHE["uT"] = np.ascontiguousarray(inp["peer_u"].transpose(0, 2, 1))
    _const_tables()
    in_maps = [prepare_inputs(inp, c) for c in range(8)]
    res = run_bass_kernel_spmd(nc, in_maps, core_ids=list(range(8)))
    r = res.results
    f = np.float32
    y_prompt = np.concatenate([unfm(r[c]["yp"]).reshape(4, 256, D) for c in range(8)], axis=0).astype(f)
    y_sample = np.stack([unfm(r[2 * b]["ys"]) for b in range(4)], axis=0).astype(f)
    cat = lambda name, shp: np.concatenate([np.asarray(r[c][name]).reshape((4,) + shp) for c in range(8)], axis=0).astype(f)
    new_a_k = cat("na_k", (1, 256, 2, 128))
    new_a_v = cat("na_v", (1, 256, 2, 128))
    new_b_k = cat("nb_k", (1, 256, 4, 2, 128))
    new_b_v = cat("nb_v", (1, 256, 4, 256))
    new_ssd_fwd = cat("ns_f", (1, 64, 64, 128))
    new_ssd_bwd = cat("ns_b", (1, 64, 64, 128))
    return (y_prompt, y_sample, new_a_k, new_a_v, new_b_k, new_b_v, new_ssd_fwd, new_ssd_bwd)
```

```python
import math
import numpy as np
import concourse.bass as bass
import concourse.mybir as mybir
from concourse.bass_utils import run_bass_kernel_spmd
from contextlib import ExitStack

F32 = mybir.dt.float32
BF16 = mybir.dt.bfloat16
AF = mybir.ActivationFunctionType
ALU = mybir.AluOpType
AX = mybir.AxisListType

D = 2048
NCH = 16
NT = 1024
EPS = 1e-6
NEG = -1.0e30
PEER_E = 16384

ENGS = ("pe", "act", "dve", "pool", "sp")


class Buf:
    __slots__ = ("ap", "name", "wr", "rd", "dsem", "dcnt", "multi")

    def __init__(self, ap, name, multi=False):
        self.ap = ap
        self.name = name
        self.wr = []
        self.rd = []
        self.dsem = None
        self.dcnt = 0
        self.multi = multi

    def __getitem__(self, idx):
        return View(self, self.ap[idx])


class View:
    __slots__ = ("buf", "ap")

    def __init__(self, buf, ap):
        self.buf = buf
        self.ap = ap

    def __getitem__(self, idx):
        return View(self.buf, self.ap[idx])


def _b(x):
    return x.buf if isinstance(x, View) else x


def A(x):
    if isinstance(x, (View, Buf)):
        return x.ap
    return x


class Prog:
    def __init__(self, nc):
        self.nc = nc
        self.es = ExitStack()
        self.q = {e: [] for e in ENGS}
        self.sems = {}
        self.cnt = {e: 0 for e in ENGS}
        self.seen = {e: {} for e in ENGS}
        self.pending = {e: False for e in ENGS}
        self.out_tokens = []
        self.nbuf = 0
        for e in ("pe", "act", "dve", "pool"):
            self.sems[e] = self.es.enter_context(nc.semaphore("s_" + e))

    def sbuf(self, name, shape, dt):
        t = self.es.enter_context(self.nc.sbuf_tensor(name, list(shape), dt))
        return Buf(t[:], name)

    def psum(self, name, shape, dt=F32):
        t = self.es.enter_context(self.nc.psum_tensor(name, list(shape), dt))
        return Buf(t[:], name)

    def sub(self, view, name="sub"):
        return Buf(A(view), name)

    def alias(self, new, olds):
        for o in olds:
            new.rd.extend(o.wr)
            new.rd.extend(o.rd)
        new.rd = self._maxtoks(new.rd)
        return new

    def dsem(self, buf):
        if buf.dsem is None:
            key = "d%d" % self.nbuf
            self.nbuf += 1
            self.sems[key] = self.es.enter_context(self.nc.semaphore(key))
            buf.dsem = key
        return buf.dsem

    def _need(self, eng, tok, waits):
        if tok is None:
            return
        k, v = tok
        if k == "pe" and eng == "pe":
            return
        if self.seen[eng].get(k, 0) >= v:
            return
        if waits.get(k, 0) < v:
            waits[k] = v

    def _deps(self, eng, reads, writes):
        waits = {}
        for r in reads:
            for t in _b(r).wr:
                self._need(eng, t, waits)
        for w in writes:
            b = _b(w)
            for t in b.wr:
                self._need(eng, t, waits)
            for t in b.rd:
                self._need(eng, t, waits)
        for k, v in waits.items():
            self.seen[eng][k] = v
        return list(waits.items())

    def op(self, eng, fn, reads=(), writes=(), inc=True):
        reads = [r for r in reads if isinstance(r, (Buf, View))]
        writes = [w for w in writes if isinstance(w, (Buf, View))]
        waits = self._deps(eng, reads, writes)
        tick = self.cnt[eng] + 1
        if inc:
            self.cnt[eng] = tick
            self.pending[eng] = False
        else:
            self.pending[eng] = True
        tok = (eng, tick)
        for r in reads:
            b = _b(r)
            if len(b.rd) > 64:
                b.rd = b.rd[-48:] + self._maxtoks(b.rd[:-48])
            b.rd.append(tok)
        for w in writes:
            b = _b(w)
            b.wr = [tok]
            b.rd = []
        self.q[eng].append((waits, fn, (eng, 1) if inc else None))

    @staticmethod
    def _maxtoks(toks):
        m = {}
        for k, v in toks:
            if m.get(k, 0) < v:
                m[k] = v
        return list(m.items())

    def dma(self, eng, out, in_, **kw):
        ob, ib = _b(out), _b(in_)
        if isinstance(ob, Buf) and not ob.multi:
            tb = ob
        else:
            tb = ib
        key = self.dsem(tb)
        waits = self._deps(eng, [ib] if isinstance(ib, Buf) else [], [ob] if isinstance(ob, Buf) else [])
        tb.dcnt += 16
        tok = (key, tb.dcnt)
        if isinstance(ib, Buf):
            ib.rd.append(tok)
        if isinstance(ob, Buf):
            if ob.multi:
                ob.wr = self._maxtoks(ob.wr + [tok])
            else:
                ob.wr = [tok]
                ob.rd = []
        oa, ia = A(out), A(in_)
        self.q[eng].append((waits, lambda e: e.dma_start(out=oa, in_=ia, **kw), (key, 16)))
        return tok

    def emit(self):
        nc = self.nc
        final_waits = {}
        for k, v in self.out_tokens:
            final_waits[k] = max(final_waits.get(k, 0), v)
        for e in ("pe", "act", "dve", "pool"):
            assert not self.pending[e], "engine %s ends with a non-inc instruction" % e
            if self.cnt[e]:
                final_waits[e] = self.cnt[e]
        self.q["sp"].append((list(final_waits.items()), None, None))
        sems, q = self.sems, self.q

        def run(engname):
            def body(e):
                for waits, fn, inc in q[engname]:
                    for k, v in waits:
                        e.wait_ge(sems[k], v)
                    if fn is None:
                        continue
                    ins = fn(e)
                    if inc is not None:
                        ins.then_inc(sems[inc[0]], inc[1])
            return body

        with nc.Block() as block:
            block.tensor(run("pe"))
            block.scalar(run("act"))
            block.vector(run("dve"))
            block.gpsimd(run("pool"))
            block.sync(run("sp"))
        self.es.close()


class K:
    def __init__(self, nc, flags):
        self.nc = nc
        self.p = Prog(nc)
        self.flags = flags
        self.ins = {}
        self.outs = {}
        self.pb_i = 0

    def inp(self, name, shape):
        self.ins[name] = self.nc.dram_tensor(name, list(shape), F32, kind="ExternalInput").ap()
        return self.ins[name]

    def outp(self, name, shape):
        self.outs[name] = self.nc.dram_tensor(name, list(shape), F32, kind="ExternalOutput").ap()
        return self.outs[name]

    def mm(self, ps, lhsT, rhs, start, stop, inc=None):
        if inc is None:
            inc = stop
        pa, la, ra = A(ps), A(lhsT), A(rhs)
        self.p.op("pe", lambda e: e.matmul(pa, lhsT=la, rhs=ra, start=start, stop=stop),
                  reads=[lhsT, rhs], writes=[ps], inc=inc)

    def act(self, out, in_, func, bias=None, scale=None, reads=(), accum_out=None):
        oa, ia = A(out), A(in_)
        kw = {}
        if bias is not None:
            kw["bias"] = A(bias)
        if scale is not None:
            kw["scale"] = A(scale)
        if accum_out is not None:
            kw["accum_out"] = A(accum_out)
        w = [out] + ([accum_out] if accum_out is not None else [])
        self.p.op("act", lambda e: e.activation(out=oa, in_=ia, func=func, **kw),
                  reads=[in_, bias, scale] + list(reads), writes=w)

    def tt(self, out, in0, in1, op, eng="dve"):
        oa, a0, a1 = A(out), A(in0), A(in1)
        self.p.op(eng, lambda e: e.tensor_tensor(out=oa, in0=a0, in1=a1, op=op), reads=[in0, in1], writes=[out])

    def ts(self, out, in0, s1, op0, s2=None, op1=None, eng="dve"):
        oa, a0 = A(out), A(in0)
        s1a, s2a = A(s1), A(s2)
        if op1 is None:
            fn = lambda e: e.tensor_scalar(out=oa, in0=a0, scalar1=s1a, scalar2=None, op0=op0)
        else:
            fn = lambda e: e.tensor_scalar(out=oa, in0=a0, scalar1=s1a, scalar2=s2a, op0=op0, op1=op1)
        self.p.op(eng, fn, reads=[in0, s1, s2], writes=[out])

    def stt(self, out, in0, scalar, in1, op0, op1):
        oa, a0, sa, a1 = A(out), A(in0), A(scalar), A(in1)
        self.p.op("dve", lambda e: e.scalar_tensor_tensor(out=oa, in0=a0, scalar=sa, in1=a1, op0=op0, op1=op1),
                  reads=[in0, scalar, in1], writes=[out])

    def copy(self, out, in_, eng="dve"):
        oa, ia = A(out), A(in_)
        if eng == "act":
            self.p.op("act", lambda e: e.copy(out=oa, in_=ia), reads=[in_], writes=[out])
        else:
            self.p.op(eng, lambda e: e.tensor_copy(out=oa, in_=ia), reads=[in_], writes=[out])

    def memset(self, out, val, eng="dve"):
        oa = A(out)
        self.p.op(eng, lambda e: e.memset(oa, val), reads=[], writes=[out])

    def bank(self):
        b = self.pb[self.pb_i % len(self.pb)]
        self.pb_i += 1
        return b


def bc(ap, shape, axis):
    idx = [slice(None)] * len(ap.shape)
    idx.insert(axis, None)
    return ap[tuple(idx)].to_broadcast(list(shape))


def build_program(flags):
    nc = bass.Bass("TRN2", target_bir_lowering=False)
    k = K(nc, flags)
    p = k.p

    has_peer = flags.get("pphase", 9) > 0 and flags.get("peer", True)
    x_in = [k.inp("xp", [128, NCH, NT]), k.inp("xs", [128, NCH, NT])]
    y_out = [k.outp("yp", [128, NCH, NT]), k.outp("ys", [128, NCH, NT])]
    cond = k.inp("cond", [128, NCH, 2])
    w_mod = k.inp("w_mod", [2, D, 6 * D]) if flags.get("dbg", 9) >= 2 else None
    b_mod = k.inp("b_mod", [128, 2, 96])
    ng = k.inp("ng", [128, 4, NCH])
    fg = k.inp("fg", [128, NCH])
    ident_in = k.inp("ident", [128, 128])
    w_q = k.inp("w_q", [2, D, D]) if has_peer else None
    skT = k.inp("skT", [2, 128, 16, 128]) if has_peer else None
    uT = k.inp("uT", [2, D, PEER_E]) if has_peer else None
    pv = k.inp("pv", [2, PEER_E, D]) if has_peer else None
    has_peer = flags.get("pphase", 9) > 0 and flags.get("peer", True)
    if has_peer:
        Gd = nc.dram_tensor("Gd", [NT, PEER_E], BF16, kind="Internal").ap()
        Gd_b = Buf(Gd, "Gd", multi=True)

    has_attn = flags.get("attn", True)
    if has_attn:
        w_in = k.inp("attn_w_in", [D, 4608])
        w_out = k.inp("attn_w_out", [D, D])
        esink_in = k.inp("esink", [128, 8])
        lamv_in = k.inp("lamv", [128, 4, 128])
        subg_in = k.inp("subg", [128, 2])
        rope_in = k.inp("rope", [2, 128, NT])
        rperm_in = k.inp("rperm", [128, 128])
        masks_in = k.inp("masks", [2, 128, 512])
        caK_in = k.inp("caK", [128, 2, 512])
        caV_in = k.inp("caV", [128, 4, 2, 128])
        cbK_in = k.inp("cbK", [128, 4, 2, 512])
        cbV_in = k.inp("cbV", [128, 4, 4, 256])
        na_k = k.outp("na_k", [NT, 256])
        na_v = k.outp("na_v", [NT, 256])
        nb_k = k.outp("nb_k", [NT, 1024])
        nb_v = k.outp("nb_v", [NT, 1024])

    has_ssd = flags.get("ssd", True)
    if has_ssd:
        sw_in = k.inp("ssd_w_in", [D, 10368])
        sw_out = k.inp("ssd_w_out", [4096, D])
        convw_in = k.inp("convw", [128, 48, 6])
        dtb_in = k.inp("dtbias", [128, 128])
        alog_in = k.inp("alog", [128, 128])
        dsk_in = k.inp("dskip", [128, 64])
        ngrp_in = k.inp("ngrp", [128, 32])
        tri_in = k.inp("tri", [4, 128, 128])
        st_in = [k.inp("st_f", [128, 64, 64]), k.inp("st_b", [128, 64, 64])]
        ns_out = [k.outp("ns_f", [4 * 64 * 64, 128]), k.outp("ns_b", [4 * 64 * 64, 128])]

    xT = p.sbuf("xT", [128, NCH, NT], F32)
    hT = p.sbuf("hT", [128, NCH, NT], BF16)
    arA = p.sbuf("arA", [128, 3 * 8192], BF16)
    arB = p.sbuf("arB", [128, 20480], BF16)
    modT = p.sbuf("modT", [128, 2, 96, 2], F32)
    bmod = p.sbuf("bmod", [128, 2, 96], F32)
    ngs = p.sbuf("ngs", [128, 4, NCH], F32)
    fgs = p.sbuf("fgs", [128, NCH], F32)
    gm = p.sbuf("gm", [128, 2, 2, 2, NCH], F32)
    scond = p.sbuf("scond", [128, NCH, 2], BF16)
    condf = p.sbuf("condf", [128, NCH, 2], F32)
    ones_bf = p.sbuf("ones_bf", [128, 128], BF16)
    ident_bf = p.sbuf("ident_bf", [128, 128], BF16)
    epsb = p.sbuf("epsb", [128, 1], F32)
    rstd = p.sbuf("rstd", [128, 512], F32)
    tmpx = [p.sbuf("tmpx%d" % i, [128, 512], F32) for i in range(2)]
    arC = p.sbuf("arC", [128, 4608], BF16)

    if has_attn:
        esink = p.sbuf("esink_sb", [128, 8], F32)
        lamv = View(tmpx[1], tmpx[1].ap.rearrange("p (a d) -> p a d", a=4))
        lamt = p.sbuf("lamt", [128, 8], F32)
        subgl = p.sbuf("subgl", [128, 2], F32)
        rt1, rt2 = tmpx[0], tmpx[1]
        stage = [rstd[:, 0:256], rstd[:, 256:512]]
    if has_ssd:
        convw = p.sbuf("convw_sb", [128, 48, 6], F32)
        dtb = p.sbuf("dtb_sb", [128, 128], F32)
        ABt = p.sbuf("ABt", [128, 128], F32)
        dsk = p.sbuf("dsk_sb", [128, 64], F32)
        ngrp = p.sbuf("ngrp_sb", [128, 32], F32)
        maskn = p.sbuf("maskn", [128, 2, 128], F32)
        identF = p.sbuf("identF", [128, 128], F32)
        onesF = p.sbuf("onesF", [128, 128], F32)
        GTs = p.sbuf("GTs", [128, 128], BF16)
    wslot = [p.sub(arA[:, i * 8192:(i + 1) * 8192], "wslot%d" % i) for i in range(3)]
    k.ws_i = 0

    def wget():
        b = wslot[k.ws_i % 3]
        k.ws_i += 1
        return b

    k.pb = [p.psum("pb%d" % i, [128, 512]) for i in range(7)]
    ptr = p.psum("ptr", [128, 1024], BF16)

    k.memset(ones_bf, 1.0)
    k.memset(epsb, EPS)
    p.dma("pool", ident_bf, ident_in)
    p.dma("sp", bmod, b_mod)
    p.dma("sp", ngs, ng)
    p.dma("sp", fgs, fg)
    p.dma("sp", condf, cond)

    if has_attn:
        p.dma("sp", esink, esink_in)
        p.dma("sp", lamv, lamv_in)
        p.dma("sp", subgl, subg_in)
        k.act(esink, esink, AF.Exp)
        LAM_INIT = 0.8 - 0.6 * math.exp(-0.3 * 0)
        for j in range(2):
            k.tt(rt1[:, 0:128], lamv[:, 2 * j, :], lamv[:, 2 * j + 1, :], ALU.mult)
            r1, lo = A(rt1[:, 0:128]), A(lamt[:, j:j + 1])
            p.op("dve", lambda e, r1=r1, lo=lo: e.reduce_sum(out=lo, in_=r1, axis=AX.X), reads=[rt1], writes=[lamt])
        k.act(lamt[:, 0:2], lamt[:, 0:2], AF.Exp)
        k.tt(lamt[:, 2:3], lamt[:, 1:2], lamt[:, 0:1], ALU.subtract)
        k.ts(lamt[:, 3:4], lamt[:, 2:3], -LAM_INIT, ALU.add)
        k.ts(subgl, subgl, 1.0 - LAM_INIT, ALU.mult)

    if has_ssd:
        p.dma("sp", convw, convw_in)
        p.dma("sp", dtb, dtb_in)
        p.dma("sp", ABt, alog_in)
        p.dma("sp", dsk, dsk_in)
        p.dma("sp", ngrp, ngrp_in)
        p.dma("sp", maskn, tri_in[2:4].rearrange("a p t -> p a t"))
        p.dma("sp", identF, ident_in)
        k.memset(onesF, 1.0)
        k.act(ABt, ABt, AF.Exp)
        k.ts(ABt, ABt, -1.0, ALU.mult)

    if "silu" not in flags.get("skip", ()):
        k.act(scond, condf, AF.Silu)
    for l in range(2):
        if flags.get("dbg", 9) < 2:
            break
        pm = k.bank()
        for blk in range(24):
            w = wget()
            wv = w.ap.rearrange("p (c j) -> p c j", c=16)
            p.dma("pool", View(w, wv), w_mod[l, :, blk * 512:(blk + 1) * 512].rearrange("(c p) j -> p c j", p=128))
            for jb in range(4):
                jc = blk * 4 + jb
                for dc in range(NCH):
                    k.mm(View(pm, pm.ap[:, jc * 2:jc * 2 + 2]), View(w, wv[:, dc, jb * 128:(jb + 1) * 128]),
                         scond[:, dc, :], start=(dc == 0), stop=(dc == NCH - 1))
        pmv = pm.ap[:, 0:192].rearrange("p (j c) -> p j c", c=2)
        k.tt(modT[:, l], View(pm, pmv), View(bmod, bc(bmod.ap[:, l], [128, 96, 2], 2)), ALU.add)
    for ci in range(2):
        if flags.get("dbg", 9) < 2:
            break
        for l in range(2):
            for kk in range(2):
                sc = modT[:, l, 16 + 48 * kk:32 + 48 * kk, ci]
                k.stt(gm[:, ci, l, kk], sc, 1.0, ngs[:, l * 2 + kk], ALU.add, ALU.mult)

    def mod_vec(ci, l, which):
        return modT[:, l, which * 16:(which + 1) * 16, ci]

    def rms_stats(blk):
        sq = p.sub(arB[:, 0:8192], "sq")
        p.alias(sq, [arB])
        sqv = View(sq, sq.ap.rearrange("p (c t) -> p c t", c=NCH))
        k.act(sqv, xT[:, :, blk * 512:(blk + 1) * 512], AF.Square)
        ps = k.bank()
        for dc in range(NCH):
            k.mm(ps, ones_bf, sqv[:, dc, :], start=(dc == 0), stop=(dc == NCH - 1))
        p.alias(arB, [sq])
        if "sqrt" in flags.get("skip", ()):
            k.copy(rstd, ps)
        else:
            k.act(rstd, ps, AF.Sqrt, bias=epsb, scale=1.0 / D)
        rs = A(rstd)
        if "recip" not in flags.get("skip", ()):
            p.op("dve", lambda e: e.reciprocal(out=rs, in_=rs), reads=[rstd], writes=[rstd])

    def adaln(ci, l, kk):
        sh = mod_vec(ci, l, 3 * kk)
        for blk in range(2):
            rms_stats(blk)
            for dc in range(NCH):
                t = tmpx[dc % 2]
                k.stt(t, xT[:, dc, blk * 512:(blk + 1) * 512], gm[:, ci, l, kk, dc:dc + 1], rstd, ALU.mult, ALU.mult)
                k.act(hT[:, dc, blk * 512:(blk + 1) * 512], t, AF.Identity, bias=sh[:, dc:dc + 1])

    def final_norm(dst):
        sk = flags.get("skip", ())
        for blk in range(2):
            if "rms" not in sk:
                rms_stats(blk)
            for dc in range(NCH):
                t = tmpx[dc % 2]
                if "stt" in sk:
                    k.copy(t, xT[:, dc, blk * 512:(blk + 1) * 512])
                else:
                    k.stt(t, xT[:, dc, blk * 512:(blk + 1) * 512], fgs[:, dc:dc + 1], rstd, ALU.mult, ALU.mult)
                tok = p.dma("sp", dst[:, dc, blk * 512:(blk + 1) * 512], t)
                p.out_tokens.append(tok)

    def peer(ci, l):
        g2 = mod_vec(ci, l, 5)
        skb = Buf(arC.ap[:, 0:2048].rearrange("p (h k) -> p h k", h=16), "skb")
        s2row = Buf(arC.ap[:, 2048:2304].bitcast(F32), "s2row")
        candh = Buf(arC.ap[:, 2304:2816].bitcast(F32), "candh")
        cand2h = Buf(arC.ap[:, 2816:3328].bitcast(F32), "cand2h")
        sv = Buf(arC.ap[:, 3328:3840].bitcast(F32).rearrange("p (h k) -> p h k", h=16), "sv")
        cv = Buf(arC.ap[:, 3840:4096].bitcast(F32).rearrange("p (h k) -> p h k", h=8), "cv")
        dtmp = Buf(arC.ap[:, 4096:4352].bitcast(F32).rearrange("p (h k) -> p h k", h=8), "dtmp")
        etmp = Buf(arC.ap[:, 4352:4416].bitcast(F32).rearrange("p (h k) -> p h k", h=8), "etmp")
        pcs = [skb, s2row, candh, cand2h, sv, cv, dtmp, etmp]
        for b_ in pcs:
            p.alias(b_, [arC])
        qT = p.sub(arB[:, 0:16384], "qT")
        s_sb = Buf(arB.ap[:, 16384:20480].bitcast(F32).rearrange("p (h k) -> p h k", h=16), "s_sb")
        p.alias(qT, [arB])
        p.alias(s_sb, [arB])
        qTv = qT.ap.rearrange("p (h t) -> p h t", h=16)
        p.dma("pool", skb, skT[l])
        for hb in range(4):
            w = wget()
            wv = w.ap.rearrange("p (c j) -> p c j", c=16)
            p.dma("pool", View(w, wv), w_q[l, :, hb * 512:(hb + 1) * 512].rearrange("(c p) j -> p c j", p=128))
            for hj in range(4):
                hc = hb * 4 + hj
                for blk in range(2):
                    ps = k.bank()
                    for dc in range(NCH):
                        k.mm(ps, View(w, wv[:, dc, hj * 128:(hj + 1) * 128]), hT[:, dc, blk * 512:(blk + 1) * 512],
                             start=(dc == 0), stop=(dc == NCH - 1))
                    k.copy(View(qT, qTv[:, hc, blk * 512:(blk + 1) * 512]), ps, eng="act")
        if flags.get("pphase", 9) < 2:
            p.alias(arB, [qT, s_sb])
            p.alias(arC, pcs)
            return
        NB3 = 3
        Cf = [Buf(arA.ap[:, i * 4096:(i + 1) * 4096].bitcast(F32), "Cf%d" % i) for i in range(NB3)]
        Eb = [p.sub(arA[:, 12288 + i * 2048:12288 + (i + 1) * 2048], "Eb%d" % i) for i in range(NB3)]
        Gc = [p.sub(arA[:, 18432 + i * 2048:18432 + (i + 1) * 2048], "Gc%d" % i) for i in range(2)]
        for b_ in Cf + Eb + Gc:
            p.alias(b_, wslot)
        for tt_ in range(flags.get("ntt", 8)):
            tsl = slice(tt_ * 128, (tt_ + 1) * 128)
            banks = [k.bank() for _ in range(4)]
            for hc in range(16):
                k.mm(View(banks[hc // 4], banks[hc // 4].ap[:, (hc % 4) * 128:(hc % 4 + 1) * 128]),
                     View(qT, qTv[:, hc, tsl]), skb[:, hc, :], start=True, stop=True)
            for b4 in range(4):
                k.copy(s_sb[:, b4 * 4:(b4 + 1) * 4, :],
                       View(banks[b4], banks[b4].ap.rearrange("p (a b) -> p a b", a=4)), eng="act")
            for hc in range(16):
                sa, s2a, sva = A(s_sb[:, hc, :]), s2row.ap, sv.ap
                p.op("dve", lambda e, sa=sa, o=sva[:, hc, 0:8]: e.max(out=o, in_=sa), reads=[s_sb], writes=[sv])
                p.op("dve", lambda e, sa=sa, s2a=s2a, o=sva[:, hc, 0:8]: e.match_replace(
                    out=s2a, in_to_replace=o, in_values=sa, imm_value=NEG), reads=[s_sb, sv], writes=[s2row])
                p.op("dve", lambda e, s2a=s2a, o=sva[:, hc, 8:16]: e.max(out=o, in_=s2a), reads=[s2row], writes=[sv])
            for h in range(8):
                c3 = candh.ap.rearrange("p (i j) -> p i j", i=16)
                k.tt(View(candh, c3), View(sv, bc(sv.ap[:, 2 * h, :], [128, 16, 16], 2)),
                     View(sv, bc(sv.ap[:, 2 * h + 1, :], [128, 16, 16], 1)), ALU.add)
                ca, c2a, cva = candh.ap, cand2h.ap, cv.ap
                p.op("dve", lambda e, ca=ca, o=cva[:, h, 0:8]: e.max(out=o, in_=ca), reads=[candh], writes=[cv])
                p.op("dve", lambda e, ca=ca, c2a=c2a, o=cva[:, h, 0:8]: e.match_replace(
                    out=c2a, in_to_replace=o, in_values=ca, imm_value=NEG), reads=[candh, cv], writes=[cand2h])
                p.op("dve", lambda e, c2a=c2a, o=cva[:, h, 8:16]: e.max(out=o, in_=c2a), reads=[cand2h], writes=[cv])
            k.tt(dtmp, cv, View(cv, bc(cv.ap[:, :, 0], [128, 8, 16], 2)), ALU.subtract)
            k.act(dtmp, dtmp, AF.Exp)
            da, ea = A(dtmp), A(etmp[:, :, 0])
            p.op("dve", lambda e, da=da, ea=ea: e.reduce_sum(out=ea, in_=da, axis=AX.X), reads=[dtmp], writes=[etmp])
            k.act(etmp[:, :, 1], etmp[:, :, 0], AF.Ln)
            k.stt(etmp[:, :, 2], etmp[:, :, 1], -1.0, cv[:, :, 0], ALU.mult, ALU.subtract)
            def emit_add(it_):
                ic_, h_ = it_ // 8, it_ % 8
                Cc_ = Cf[it_ % NB3]
                s1b = View(s_sb, bc(s_sb.ap[:, 2 * h_, ic_ * 16:(ic_ + 1) * 16], [128, 16, 128], 2))
                s2b = View(s_sb, bc(s_sb.ap[:, 2 * h_ + 1, :], [128, 16, 128], 1))
                c3 = Cc_.ap.rearrange("p (i j) -> p i j", i=16)
                k.tt(View(Cc_, c3), s1b, s2b, ALU.add,
                     eng=("pool" if (it_ % 2 == 0 and flags.get("pooladd", True)) else "dve"))

            emit_add(0)
            for it in range(64):
                ic, h = it // 8, it % 8
                gcb = Gc[ic % 2]
                Cc, Ee = Cf[it % NB3], Eb[it % NB3]
                if it + 1 < 64:
                    emit_add(it + 1)
                k.act(Ee, Cc, AF.Exp, bias=etmp[:, h, 2:3])
                k.stt(Ee, Cc, cv[:, h, 15:16], Ee, ALU.is_ge, ALU.mult)
                for q4 in range(4):
                    k.mm(k.pb[q4], ident_bf, Ee[:, q4 * 512:(q4 + 1) * 512], h == 0, h == 7, inc=(q4 == 3))
                if h == 7:
                    for q4 in range(4):
                        k.copy(gcb[:, q4 * 512:(q4 + 1) * 512], k.pb[q4], eng="act")
                    p.dma("sp", View(Gd_b, Gd[tsl, ic * 2048:(ic + 1) * 2048]), gcb)
        for w_ in wslot:
            p.alias(w_, Cf + Eb + Gc)
        if flags.get("pphase", 9) < 3:
            p.alias(arB, [qT, s_sb])
            p.alias(arC, pcs)
            return
        Gt = [p.sub(arB[:, i * 4096:(i + 1) * 4096], "Gt%d" % i) for i in range(2)]
        WT = [p.sub(arB[:, 8192 + i * 4096:8192 + (i + 1) * 4096], "WT%d" % i) for i in range(2)]
        actb = [p.sub(arB[:, 16384 + i * 512:16384 + (i + 1) * 512], "actb%d" % i) for i in range(4)]
        for b_ in Gt + WT:
            p.alias(b_, [qT])
        for b_ in actb:
            p.alias(b_, [s_sb])
        ptrs = [p.sub(ptr[:, i * 512:(i + 1) * 512], "ptrs%d" % i) for i in range(2)]
        for b_ in ptrs:
            p.alias(b_, [ptr])
        for eb in range(flags.get("neb", 32)):
            e0 = eb * 512
            wu = wget()
            wuv = wu.ap.rearrange("p (c j) -> p c j", c=16)
            p.dma("pool", View(wu, wuv), uT[l, :, e0:e0 + 512].rearrange("(c p) j -> p c j", p=128))
            wv_ = wget()
            wvv = wv_.ap.rearrange("p (c j) -> p c j", c=4)
            p.dma("pool", View(wv_, wvv), pv[l, e0:e0 + 512, :].rearrange("(c p) j -> p c j", p=128))
            gt = Gt[eb % 2]
            gtv = gt.ap.rearrange("p (a j) -> p a j", a=8)
            p.dma("sp", View(gt, gtv), View(Gd_b, Gd[:, e0:e0 + 512].rearrange("(a p) j -> p a j", p=128)))
            wt = WT[eb % 2]
            wtv = wt.ap.rearrange("p (c t) -> p c t", c=4)
            def do_transposes(tt_):
                tsl = slice(tt_ * 128, (tt_ + 1) * 128)
                ab = actb[tt_ % 4]
                pt_ = ptrs[tt_ % 2]
                for c4 in range(4):
                    pa, ia, ida = pt_.ap[:, c4 * 128:(c4 + 1) * 128], A(ab[:, c4 * 128:(c4 + 1) * 128]), ident_bf.ap
                    p.op("pe", lambda e, pa=pa, ia=ia, ida=ida: e.transpose(pa, ia, ida),
                         reads=[ab, ident_bf], writes=[pt_], inc=(c4 == 3))
                k.copy(View(wt, wtv[:, :, tsl]), View(pt_, pt_.ap.rearrange("p (c t) -> p c t", c=4)), eng="act")

            for tt_ in range(8):
                tsl = slice(tt_ * 128, (tt_ + 1) * 128)
                ps = k.bank()
                for dc in range(NCH):
                    k.mm(ps, hT[:, dc, tsl], View(wu, wuv[:, dc, :]), start=(dc == 0), stop=(dc == NCH - 1))
                ab = actb[tt_ % 4]
                k.act(ab, ps, AF.Gelu_apprx_tanh)
                k.tt(ab, ab, View(gt, gtv[:, tt_, :]), ALU.mult)
                if tt_ >= 1:
                    do_transposes(tt_ - 1)
            do_transposes(7)
            for dc in range(NCH):
                for blk in range(2):
                    ps = k.bank()
                    for c4 in range(4):
                        k.mm(ps, View(wv_, wvv[:, c4, dc * 128:(dc + 1) * 128]),
                             View(wt, wtv[:, c4, blk * 512:(blk + 1) * 512]), start=(c4 == 0), stop=(c4 == 3))
                    xv = xT[:, dc, blk * 512:(blk + 1) * 512]
                    k.stt(xv, ps, g2[:, dc:dc + 1], xv, ALU.mult, ALU.add)
        p.alias(arB, Gt + WT + actb)
        p.alias(arC, pcs)
        p.alias(ptr, ptrs)


    SCALE = 1.0 / math.sqrt(128.0)

    def attention(ci, ps_):
        sample = (ps_ == 1)
        g1 = mod_vec(ci, 0, 2)
        Qb = p.sub(arB[:, 0:4096], "Qb")
        Ktb = p.sub(arB[:, 4096:6144], "Ktb")
        Vb = p.sub(arB[:, 6144:8192], "Vb")
        mixb = p.sub(arB[:, 8192:12288], "mixb")
        PT = [p.sub(arB[:, 12288 + i * 512:12800 + i * 512], "PT%d" % i) for i in range(2)]
        O1n = Buf(arB.ap[:, 13312:15360].bitcast(F32).rearrange("p (a t) -> p a t", a=2), "O1n")
        tmpO = Buf(arB.ap[:, 15360:16384].bitcast(F32), "tmpO")
        sqd = Buf(arB.ap[:, 16384:17408].rearrange("p (a t) -> p a t", a=2), "sqd")
        cK = p.sub(arB[:, 17408:18432], "cK")
        cV = p.sub(arB[:, 18432:19456], "cV")
        recs = Buf(arB.ap[:, 19456:20480].bitcast(F32), "recs")
        allb = [Qb, Ktb, Vb, mixb] + PT + [O1n, tmpO, sqd, cK, cV, recs]
        for b_ in allb:
            p.alias(b_, [arB])
        ropeT = Buf(arC.ap[:, 0:2048].rearrange("p (a t) -> p a t", a=2), "ropeT")
        masks = Buf(arC.ap[:, 2048:3072].rearrange("p (a t) -> p a t", a=2), "masks")
        xb = Buf(arC.ap[:, 3072:3584], "xb")
        rperm = Buf(arC.ap[:, 3584:3712], "rperm")
        acs = [ropeT, masks, xb, rperm]
        for b_ in acs:
            p.alias(b_, [arC])
        if sample:
            p.dma("pool", ropeT, rope_in.rearrange("a p t -> p a t"))
            p.dma("pool", rperm, rperm_in)
            p.dma("pool", masks, masks_in.rearrange("a p t -> p a t"))
        sbk = [k.pb[0], k.pb[1], k.pb[2]]
        obk = [k.pb[3], k.pb[4]]
        smk = k.pb[5]
        msk = k.pb[6]
        st_i = [0]

        def rope_tile(dst, ps, blk):
            k.copy(xb, ps, eng="act")
            k.mm(msk, rperm, xb, True, True)
            a0, a1, a2 = A(rt1), A(ps), A(ropeT[:, 0, blk * 512:(blk + 1) * 512])
            p.op("dve", lambda e, a0=a0, a1=a1, a2=a2: e.tensor_tensor(out=a0, in0=a1, in1=a2, op=ALU.mult),
                 reads=[ps, ropeT, msk], writes=[rt1])
            k.tt(rt2, msk, ropeT[:, 1, blk * 512:(blk + 1) * 512], ALU.mult)
            k.tt(dst, rt1, rt2, ALU.add)

        def proj_fm(dst, wcols, rope):
            for blk in range(2):
                ps = sbk[st_i[0] % 3]
                st_i[0] += 1
                for dc in range(NCH):
                    k.mm(ps, wcols[:, dc, :], hT[:, dc, blk * 512:(blk + 1) * 512], dc == 0, dc == NCH - 1)
                if rope:
                    rope_tile(dst[:, blk * 512:(blk + 1) * 512], ps, blk)
                else:
                    k.copy(dst[:, blk * 512:(blk + 1) * 512], ps, eng="act")

        def proj_tm(wcols, ncols, tt_):
            ps = sbk[st_i[0] % 3]
            st_i[0] += 1
            for dc in range(NCH):
                k.mm(ps[:, 0:ncols], hT[:, dc, tt_ * 128:(tt_ + 1) * 128], wcols[:, dc, :], dc == 0, dc == NCH - 1)
            return ps

        def core(qv, N, chunks, ndvt):
            nchk = len(chunks)
            for ci_, (kt, vs, mask) in enumerate(chunks):
                sp = sbk[st_i[0] % 3]
                st_i[0] += 1
                k.mm(sp[:, 0:N], kt, qv, True, mask is None)
                if mask is not None:
                    k.mm(sp[:, 0:N], ident_bf, mask[:, 0:N], False, True)
                pt = PT[ci_ % 2]
                k.act(pt[:, 0:N], sp[:, 0:N], AF.Exp, scale=SCALE)
                for dvt in range(ndvt):
                    k.mm(obk[dvt][:, 0:N], vs[dvt], pt[:, 0:N], ci_ == 0, ci_ == nchk - 1)
                k.mm(smk[:, 0:N], ones_bf, pt[:, 0:N], ci_ == 0, ci_ == nchk - 1)

        def wout_partial(rows0, nh):
            wo = wget()
            wov = wo.ap[:, 0:nh * 2048].rearrange("p (h j) -> p h j", h=nh)
            p.dma("pool", View(wo, wov), w_out[rows0:rows0 + nh * 128, :].rearrange("(h p) j -> p h j", p=128))
            mv = mixb.ap[:, 0:nh * 1024].rearrange("p (h t) -> p h t", h=nh)
            for dc in range(NCH):
                for blk in range(2):
                    ps = sbk[st_i[0] % 3]
                    st_i[0] += 1
                    for h in range(nh):
                        k.mm(ps, View(wo, wov[:, h, dc * 128:(dc + 1) * 128]),
                             View(mixb, mv[:, h, blk * 512:(blk + 1) * 512]), h == 0, h == nh - 1)
                    xv = xT[:, dc, blk * 512:(blk + 1) * 512]
                    k.stt(xv, ps, g1[:, dc:dc + 1], xv, ALU.mult, ALU.add)

        for g in flags.get('agroups', (0, 1)):
            wq = wget()
            wqv = wq.ap.rearrange("p (c j) -> p c j", c=16)
            p.dma("pool", View(wq, wqv), w_in[:, g * 512:(g + 1) * 512].rearrange("(c p) j -> p c j", p=128))
            wkv = wget()
            wkvv = wkv.ap[:, 0:4096].rearrange("p (c j) -> p c j", c=16)
            p.dma("pool", View(wkv, wkvv[:, :, 0:128]),
                  w_in[:, 1024 + g * 128:1152 + g * 128].rearrange("(c p) j -> p c j", p=128))
            p.dma("pool", View(wkv, wkvv[:, :, 128:256]),
                  w_in[:, 1280 + g * 128:1408 + g * 128].rearrange("(c p) j -> p c j", p=128))
            Qv = Qb.ap.rearrange("p (h t) -> p h t", h=4)
            Ktv = Ktb.ap[:, 0:1024]
            Vv = Vb.ap[:, 0:1024].rearrange("p (a d) -> p a d", a=8)
            mv = mixb.ap.rearrange("p (h t) -> p h t", h=4)
            for h in range(flags.get("dq", 4)):
                proj_fm(View(Qb, Qv[:, h, :]), View(wq, wqv[:, :, h * 128:(h + 1) * 128]), sample)
            if flags.get("dk", 1):
                proj_fm(View(Ktb, Ktv), View(wkv, wkvv[:, :, 0:128]), sample)
            for tt_ in range(flags.get("dtm", 8)):
                if sample:
                    ps = proj_tm(View(wkv, wkvv[:, :, 128:256]), 128, tt_)
                    k.copy(View(Vb, Vv[:, tt_, :]), ps[:, 0:128])
                else:
                    ps = proj_tm(View(wkv, wkvv[:, :, 0:256]), 256, tt_)
                    sg = stage[tt_ % 2]
                    if flags.get("x1", 1):
                        k.copy(sg, ps[:, 0:256], eng="act")
                    if flags.get("x2", 1):
                        k.copy(View(Vb, Vv[:, tt_, :]), sg[:, 128:256])
                    for (dst, c0) in ((na_k, 0), (na_v, 128)):
                        if not flags.get("x3", 1):
                            continue
                        tok = p.dma("sp", dst[tt_ * 128:(tt_ + 1) * 128, g * 128:(g + 1) * 128], sg[:, c0:c0 + 128])
                        p.out_tokens.append(tok)
            if sample:
                cKv = cK.ap[:, 0:512]
                cVv = cV.ap[:, 0:512].rearrange("p (a d) -> p a d", a=4)
                p.dma("pool", View(cK, cKv), caK_in[:, g, :])
                p.dma("pool", View(cV, cVv), caV_in[:, :, g, :])
            nqb = flags.get('nqb', 8)
            for qb in range(nqb):
                t0 = qb * 128
                qv = View(Qb, Qv[:, :, t0:t0 + 128])
                chunks = []
                if sample:
                    for kc in range(4):
                        chunks.append((View(cK, cKv[:, kc * 128:(kc + 1) * 128]), [View(cV, cVv[:, kc, :])], None))
                    for nb in (qb - 1, qb, qb + 1):
                        if 0 <= nb < 8:
                            m_ = None if nb == qb else (masks[:, 0, :] if nb == qb - 1 else masks[:, 1, :])
                            chunks.append((View(Ktb, Ktv[:, nb * 128:(nb + 1) * 128]), [View(Vb, Vv[:, nb, :])], m_))
                else:
                    s_ = qb // 2
                    for kc in range(2):
                        tk = s_ * 2 + kc
                        chunks.append((View(Ktb, Ktv[:, tk * 128:(tk + 1) * 128]), [View(Vb, Vv[:, tk, :])], None))
                core(qv, 512, chunks, 1)
                for h in range(4):
                    k.ts(smk[:, h * 128:(h + 1) * 128], smk[:, h * 128:(h + 1) * 128],
                         esink[:, g * 4 + h:g * 4 + h + 1], ALU.add)
                ra, sa_ = recs.ap, smk.ap
                p.op("dve", lambda e, ra=ra, sa_=sa_: e.reciprocal(out=ra, in_=sa_), reads=[smk], writes=[recs])
                k.tt(View(mixb, mv[:, :, t0:t0 + 128]), View(obk[0], obk[0].ap.rearrange("p (h t) -> p h t", h=4)),
                     View(recs, recs.ap.rearrange("p (h t) -> p h t", h=4)), ALU.mult)
            if flags.get('awout', True):
                wout_partial(g * 512, 4)

        for h in flags.get('bheads', (0, 1, 2, 3)):
            wqk = wget()
            wqkv = wqk.ap.rearrange("p (c j) -> p c j", c=16)
            p.dma("pool", View(wqk, wqkv[:, :, 0:256]),
                  w_in[:, 1536 + h * 256:1792 + h * 256].rearrange("(c p) j -> p c j", p=128))
            p.dma("pool", View(wqk, wqkv[:, :, 256:512]),
                  w_in[:, 2560 + h * 256:2816 + h * 256].rearrange("(c p) j -> p c j", p=128))
            wv_ = wget()
            wvv = wv_.ap[:, 0:4096].rearrange("p (c j) -> p c j", c=16)
            p.dma("pool", View(wv_, wvv), w_in[:, 3584 + h * 256:3840 + h * 256].rearrange("(c p) j -> p c j", p=128))
            Qv = Qb.ap[:, 0:2048].rearrange("p (j t) -> p j t", j=2)
            Ktv = Ktb.ap.rearrange("p (j t) -> p j t", j=2)
            Vv = Vb.ap.rearrange("p (a d) -> p a d", a=8)
            mv = mixb.ap[:, 0:2048].rearrange("p (a t) -> p a t", a=2)
            for j in range(2):
                proj_fm(View(Qb, Qv[:, j, :]), View(wqk, wqkv[:, :, j * 128:(j + 1) * 128]), sample)
                proj_fm(View(Ktb, Ktv[:, j, :]), View(wqk, wqkv[:, :, 256 + j * 128:384 + j * 128]), sample)
            for tt_ in range(8):
                ps = proj_tm(View(wv_, wvv), 256, tt_)
                if sample:
                    k.copy(View(Vb, Vv[:, tt_, :]), ps[:, 0:256])
                if not sample:
                    sg = stage[0]
                    k.copy(sg, ps[:, 0:256], eng="act")
                    k.copy(View(Vb, Vv[:, tt_, :]), sg)
                    tok = p.dma("sp", nb_v[tt_ * 128:(tt_ + 1) * 128, h * 256:(h + 1) * 256], sg)
                    p.out_tokens.append(tok)
                    ps2 = proj_tm(View(wqk, wqkv[:, :, 256:512]), 256, tt_)
                    sg = stage[1]
                    k.copy(sg, ps2[:, 0:256], eng="act")
                    tok = p.dma("sp", nb_k[tt_ * 128:(tt_ + 1) * 128, h * 256:(h + 1) * 256], sg)
                    p.out_tokens.append(tok)
            if sample:
                cKv = cK.ap.rearrange("p (j t) -> p j t", j=2)
                cVv = cV.ap.rearrange("p (a d) -> p a d", a=4)
                p.dma("pool", View(cK, cKv), cbK_in[:, h, :, :])
                p.dma("pool", View(cV, cVv), cbV_in[:, :, h, :])
            nq = 2 if sample else 4
            nq = min(nq, flags.get('nqB', 9))
            N = 512 if sample else 256
            for qi in range(nq):
                tq = slice(qi * N, (qi + 1) * N)
                for j in range(2):
                    chunks = []
                    if sample:
                        for kc in range(4):
                            chunks.append((View(cK, cKv[:, j, kc * 128:(kc + 1) * 128]),
                                           [View(cV, cVv[:, kc, d_ * 128:(d_ + 1) * 128]) for d_ in range(2)], None))
                        for kc in range(8):
                            chunks.append((View(Ktb, Ktv[:, j, kc * 128:(kc + 1) * 128]),
                                           [View(Vb, Vv[:, kc, d_ * 128:(d_ + 1) * 128]) for d_ in range(2)], None))
                    else:
                        for kc in range(2):
                            tk = qi * 2 + kc
                            chunks.append((View(Ktb, Ktv[:, j, tk * 128:(tk + 1) * 128]),
                                           [View(Vb, Vv[:, tk, d_ * 128:(d_ + 1) * 128]) for d_ in range(2)], None))
                    core(View(Qb, Qv[:, j, tq]), N, chunks, 2)
                    ra, sa_ = recs.ap[:, 0:N], smk.ap[:, 0:N]
                    p.op("dve", lambda e, ra=ra, sa_=sa_: e.reciprocal(out=ra, in_=sa_), reads=[smk], writes=[recs])
                    for d_ in range(2):
                        if j == 0:
                            k.tt(O1n[:, d_, 0:N], obk[d_][:, 0:N], recs[:, 0:N], ALU.mult)
                        else:
                            k.stt(tmpO[:, 0:N], obk[d_][:, 0:N], lamt[:, 3:4], recs[:, 0:N], ALU.mult, ALU.mult)
                            k.tt(O1n[:, d_, 0:N], O1n[:, d_, 0:N], tmpO[:, 0:N], ALU.add)
                for d_ in range(2):
                    k.act(sqd[:, d_, 0:N], O1n[:, d_, 0:N], AF.Square)
                for d_ in range(2):
                    k.mm(msk[:, 0:N], ones_bf, sqd[:, d_, 0:N], d_ == 0, d_ == 1)
                k.act(recs[:, 0:N], msk[:, 0:N], AF.Sqrt, bias=epsb, scale=1.0 / 256.0)
                ra = recs.ap[:, 0:N]
                p.op("dve", lambda e, ra=ra: e.reciprocal(out=ra, in_=ra), reads=[recs], writes=[recs])
                for d_ in range(2):
                    k.stt(View(mixb, mv[:, d_, tq]), O1n[:, d_, 0:N], subgl[:, d_:d_ + 1], recs[:, 0:N], ALU.mult, ALU.mult)
            if flags.get('awout', True):
                wout_partial(1024 + h * 256, 2)
        p.alias(arB, allb)
        p.alias(arC, acs)


    def ssd(ci, ps_):
        sample = (ps_ == 1)
        g1 = mod_vec(ci, 1, 2)
        seqs = [(0, 8)] if sample else [(0, 2), (2, 4), (4, 6), (6, 8)]
        nseq = len(seqs)
        slen = NT // nseq
        szT = Buf(arB.ap[:, 0:4096].rearrange("p (a t) -> p a t", a=4), "szT")
        xcT = Buf(arB.ap[:, 4096:8192].rearrange("p (a t) -> p a t", a=4), "xcT")
        xtok = Buf(arB.ap[:, 8192:12288].rearrange("p (c j) -> p c j", c=8), "xtok")
        hbin = Buf(arB.ap[:, 12288:16384].rearrange("p (c j) -> p c j", c=8), "hbin")
        BT = Buf(arB.ap[:, 16384:17408], "BT")
        CT = Buf(arB.ap[:, 17408:18432], "CT")
        Btok = Buf(arB.ap[:, 18432:19456].rearrange("p (c j) -> p c j", c=8), "Btok")
        hbf = [Buf(arB.ap[:, 19456 + i * 256:19712 + i * 256], "hbf%d" % i) for i in range(2)]
        bB = [szT, xcT, xtok, hbin, BT, CT, Btok] + hbf
        for b_ in bB:
            p.alias(b_, [arB])
        wdt = Buf(arC.ap[:, 0:2048].rearrange("p (c j) -> p c j", c=16), "wdt")
        dtall = Buf(arC.ap[:, 2048:4096].bitcast(F32).rearrange("p (c j) -> p c j", c=8), "dtall")
        Tm = Buf(arC.ap[:, 4096:4608].bitcast(F32).rearrange("p (a t) -> p a t", a=2), "Tm")
        bC = [wdt, dtall, Tm]
        for b_ in bC:
            p.alias(b_, [arC])
        S2 = wslot[2]
        xpre = Buf(S2.ap[:, 0:2048].bitcast(F32), "xpre")
        acc = Buf(S2.ap[:, 2048:4096].bitcast(F32), "acc")
        aT = Buf(S2.ap[:, 0:1024].bitcast(F32).rearrange("p (j t) -> p j t", j=4), "aT")
        LT = Buf(S2.ap[:, 1024:2048].bitcast(F32).rearrange("p (j t) -> p j t", j=4), "LT")
        EB = Buf(S2.ap[:, 2048:3072].bitcast(F32).rearrange("p (j t) -> p j t", j=4), "EB")
        MTs = [Buf(S2.ap[:, 3072 + i * 512:3584 + i * 512].rearrange("p (j t) -> p j t", j=4), "MT%d" % i) for i in range(2)]
        CeTs = [Buf(S2.ap[:, 4096 + i * 512:4608 + i * 512].rearrange("p (j t) -> p j t", j=4), "CeT%d" % i) for i in range(2)]
        hst = [Buf(S2.ap[:, 5120 + i * 512:5632 + i * 512].bitcast(F32), "hst%d" % i) for i in range(2)]
        y2 = Buf(S2.ap[:, 6144:6656].bitcast(F32), "y2")
        Xs = Buf(S2.ap[:, 6656:7680].rearrange("p (a t) -> p a t", a=4), "Xs")
        sm = Buf(S2.ap[:, 7680:7808].bitcast(F32).rearrange("p (a t) -> p a t", a=8), "sm")
        convb = [xpre, acc]
        sweepb = [aT, LT, EB] + MTs + CeTs + hst + [y2, Xs, sm]
        for b_ in convb + sweepb:
            p.alias(b_, [S2])
        k.ws2 = 0

        def wget2():
            b = wslot[k.ws2 % 2]
            k.ws2 += 1
            return b

        sbk = [k.pb[0], k.pb[1], k.pb[2]]
        st_i = [0]

        def nb():
            b = sbk[st_i[0] % 3]
            st_i[0] += 1
            return b

        p.dma("pool", wdt, sw_in[:, 10240:10368].rearrange("(c p) j -> p c j", p=128))
        p.dma("sp", Tm, tri_in[0:2].rearrange("a p t -> p a t"))
        for tt_ in range(8):
            ps = nb()
            for dc in range(NCH):
                k.mm(ps[:, 0:128], hT[:, dc, tt_ * 128:(tt_ + 1) * 128], wdt[:, dc, :], dc == 0, dc == NCH - 1)
            k.tt(dtall[:, tt_, :], ps[:, 0:128], dtb, ALU.add)
        k.act(dtall, dtall, AF.Exp)
        k.act(dtall, dtall, AF.Ln, bias=1.0)

        def conv_silu(dst, tile_idx):
            k.ts(acc, xpre, convw[:, tile_idx, 2:3], ALU.mult, convw[:, tile_idx, 5:6], ALU.add)
            a3 = acc.ap.rearrange("p (s t) -> p s t", s=nseq)
            x3 = xpre.ap.rearrange("p (s t) -> p s t", s=nseq)
            for kk_ in (0, 1, 3, 4):
                sh = kk_ - 2
                o_lo, o_hi = max(0, -sh), slen - max(0, sh)
                i_lo, i_hi = max(0, sh), slen - max(0, -sh)
                k.stt(View(acc, a3[:, :, o_lo:o_hi]), View(xpre, x3[:, :, i_lo:i_hi]), convw[:, tile_idx, kk_:kk_ + 1],
                      View(acc, a3[:, :, o_lo:o_hi]), ALU.mult, ALU.add)
            k.act(dst, acc, AF.Silu)

        def proj_to_xpre(wcols):
            for blk in range(2):
                ps = nb()
                for dc in range(NCH):
                    k.mm(ps, wcols[:, dc, :], hT[:, dc, blk * 512:(blk + 1) * 512], dc == 0, dc == NCH - 1)
                k.copy(xpre[:, blk * 512:(blk + 1) * 512], ps, eng="act")

        smg = Buf(arC.ap[:, 0:1536].bitcast(F32).rearrange("p (r c j) -> p r c j", r=6, c=8), "smg")
        p.alias(smg, [wdt])
        bC.append(smg)

        def group_small(g):
            for d_ in range(2):
                cs = slice(d_ * 64 + g * 8, d_ * 64 + g * 8 + 8)
                k.copy(smg[:, 5, :, d_ * 8:(d_ + 1) * 8], dtall[:, :, cs])
                k.tt(smg[:, 0, :, d_ * 8:(d_ + 1) * 8], dtall[:, :, cs],
                     View(ABt, bc(ABt.ap[:, cs], [128, 8, 8], 1)), ALU.mult)
            psm = k.pb[5]
            pv3 = psm.ap[:, 0:256].rearrange("p (c j) -> p c j", c=8)
            for c in range(8):
                for d_ in range(2):
                    k.mm(View(psm, pv3[:, c, d_ * 8:(d_ + 1) * 8]), Tm[:, d_, :], smg[:, 0, c, d_ * 8:(d_ + 1) * 8], True, True)
                k.mm(View(psm, pv3[:, c, 16:32]), onesF, smg[:, 0, c, :], True, True)
            k.copy(smg[:, 1], View(psm, pv3[:, :, 0:16]), eng="act")
            k.copy(smg[:, 3], View(psm, pv3[:, :, 16:32]), eng="act")
            k.ts(smg[:, 1], smg[:, 1], -1.0, ALU.mult)
            k.tt(smg[:, 2], smg[:, 3], smg[:, 1], ALU.add)
            k.act(smg[:, 2], smg[:, 2], AF.Exp)
            k.act(smg[:, 3], smg[:, 3], AF.Exp)
            k.tt(smg[:, 4], smg[:, 5], smg[:, 2], ALU.mult)

        for g in flags.get("sgroups", range(8)):
            group_small(g)
            for b_ in convb:
                p.alias(b_, sweepb)
            wz = wget2()
            wzv = wz.ap.rearrange("p (c j) -> p c j", c=16)
            p.dma("pool", View(wz, wzv), sw_in[:, g * 512:(g + 1) * 512].rearrange("(c p) j -> p c j", p=128))
            for t4 in range(4):
                for blk in range(2):
                    ps = nb()
                    for dc in range(NCH):
                        k.mm(ps, View(wz, wzv[:, dc, t4 * 128:(t4 + 1) * 128]), hT[:, dc, blk * 512:(blk + 1) * 512],
                             dc == 0, dc == NCH - 1)
                    k.act(szT[:, t4, blk * 512:(blk + 1) * 512], ps, AF.Silu)
            wx = wget2()
            wxv = wx.ap.rearrange("p (c j) -> p c j", c=16)
            p.dma("pool", View(wx, wxv), sw_in[:, 4096 + g * 512:4608 + g * 512].rearrange("(c p) j -> p c j", p=128))
            for t4 in range(4):
                proj_to_xpre(View(wx, wxv[:, :, t4 * 128:(t4 + 1) * 128]))
                conv_silu(xcT[:, t4, :], g * 4 + t4)
            wbc = wget2()
            wbcv = wbc.ap[:, 0:4096].rearrange("p (c j) -> p c j", c=16)
            p.dma("pool", View(wbc, wbcv[:, :, 0:128]),
                  sw_in[:, 8192 + g * 128:8320 + g * 128].rearrange("(c p) j -> p c j", p=128))
            p.dma("pool", View(wbc, wbcv[:, :, 128:256]),
                  sw_in[:, 9216 + g * 128:9344 + g * 128].rearrange("(c p) j -> p c j", p=128))
            proj_to_xpre(View(wbc, wbcv[:, :, 0:128]))
            conv_silu(BT, 32 + g)
            proj_to_xpre(View(wbc, wbcv[:, :, 128:256]))
            conv_silu(CT, 40 + g)
            sstage = flags.get("sstage", 9)
            for c in range(8 if sstage >= 2 else 0):
                csl = slice(c * 128, (c + 1) * 128)
                for t4 in range(4):
                    pa, ia, ida = ptr.ap[:, t4 * 128:(t4 + 1) * 128], A(xcT[:, t4, csl]), ident_bf.ap
                    p.op("pe", lambda e, pa=pa, ia=ia, ida=ida: e.transpose(pa, ia, ida),
                         reads=[xcT, ident_bf], writes=[ptr], inc=False)
                pa, ia, ida = ptr.ap[:, 512:640], A(BT[:, csl]), ident_bf.ap
                p.op("pe", lambda e, pa=pa, ia=ia, ida=ida: e.transpose(pa, ia, ida),
                     reads=[BT, ident_bf], writes=[ptr], inc=True)
                k.copy(xtok[:, c, :], ptr[:, 0:512], eng="act")
                k.copy(Btok[:, c, :], ptr[:, 512:640], eng="act")
            for b_ in sweepb:
                p.alias(b_, convb)

            for hh in range(2 if sstage >= 3 else 0):
                hs0 = g * 8 + hh * 4
                cols = [slice(hs0, hs0 + 4), slice(64 + hs0, 64 + hs0 + 4)]
                xsl = slice(hh * 256, (hh + 1) * 256)

                def chunk_small(c):
                    for d_ in range(2):
                        k.tt(sm[:, 0, d_ * 4:(d_ + 1) * 4], dtall[:, c, cols[d_]], ABt[:, cols[d_]], ALU.mult)
                        k.copy(sm[:, 5, d_ * 4:(d_ + 1) * 4], dtall[:, c, cols[d_]])
                    psm = k.pb[5]
                    for d_ in range(2):
                        k.mm(psm[:, d_ * 4:(d_ + 1) * 4], Tm[:, d_, :], sm[:, 0, d_ * 4:(d_ + 1) * 4], True, True)
                    k.mm(psm[:, 8:16], onesF, sm[:, 0, :], True, True)
                    k.copy(sm[:, 6:8, :], View(psm, psm.ap[:, 0:16].rearrange("p (a t) -> p a t", a=2)), eng="act")
                    k.ts(sm[:, 1, :], sm[:, 6, :], -1.0, ALU.mult)
                    k.tt(sm[:, 2, :], sm[:, 7, :], sm[:, 1, :], ALU.add)
                    k.act(sm[:, 2, :], sm[:, 2, :], AF.Exp)
                    k.act(sm[:, 3, :], sm[:, 7, :], AF.Exp)
                    k.tt(sm[:, 4, :], sm[:, 5, :], sm[:, 2, :], ALU.mult)

                def smc(row, c, d_):
                    return smg.ap[:, row, c, d_ * 8 + hh * 4:d_ * 8 + hh * 4 + 4]

                def scaled_x(dst, c, row, d_):
                    xv = xtok.ap[:, c, xsl].rearrange("p (j q) -> p j q", j=4)
                    k.tt(View(Xs, dst.rearrange("p (j q) -> p j q", j=4)), View(xtok, xv),
                         View(smg, bc(smc(row, c, d_), [128, 4, 64], 2)), ALU.mult)

                def state_update(d_, c, xw_ap):
                    stp = k.pb[6]
                    k.mm(stp[:, 0:256], Btok[:, c, :], View(Xs, xw_ap), True, True)
                    h3 = hst[d_].ap.rearrange("p (j q) -> p j q", j=4)
                    k.tt(View(hst[d_], h3), View(hst[d_], h3),
                         View(smg, bc(smc(3, c, d_), [128, 4, 64], 2)), ALU.mult)
                    k.tt(hst[d_], hst[d_], stp[:, 0:256], ALU.add)

                def init_state(d_, si):
                    if sample:
                        p.dma("sp", View(hst[d_], hst[d_].ap.rearrange("p (j q) -> p j q", j=4)), st_in[d_][:, hs0:hs0 + 4, :])
                    else:
                        k.memset(hst[d_], 0.0)

                def out_state(d_, si):
                    if sample:
                        return
                    ptf = k.pb[6]
                    for t2 in range(2):
                        pa, ia, ida = ptf.ap[:, 256 + t2 * 128:384 + t2 * 128], A(hst[d_][:, t2 * 128:(t2 + 1) * 128]), identF.ap
                        p.op("pe", lambda e, pa=pa, ia=ia, ida=ida: e.transpose(pa, ia, ida),
                             reads=[hst[d_], identF], writes=[ptf], inc=(t2 == 1))
                    k.copy(y2, ptf[:, 256:512], eng="act")
                    for t2 in range(2):
                        r0 = (si * 64 + hs0 + t2 * 2) * 64
                        tok = p.dma("sp", ns_out[d_][r0:r0 + 128, :], y2[:, t2 * 128:(t2 + 1) * 128])
                        p.out_tokens.append(tok)

                for si, (c0, c1) in enumerate(seqs):
                    init_state(1, si)
                    for c in reversed(range(c0, c1)):
                        k.copy(hbin[:, c, xsl], hst[1], eng="act")
                        scaled_x(Xs.ap[:, 3, :], c, 4, 1)
                        state_update(1, c, Xs.ap[:, 3, :])
                    out_state(1, si)
                for si, (c0, c1) in enumerate(seqs if sstage >= 4 else []):
                    init_state(0, si)
                    for c in range(c0, c1):
                        csl = slice(c * 128, (c + 1) * 128)
                        k.copy(hbf[0], hst[0], eng="act")
                        gp = k.pb[5]
                        k.mm(gp[:, 128:256], BT[:, csl], CT[:, csl], True, True)
                        k.copy(GTs, gp[:, 128:256], eng="act")
                        for d_ in range(2):
                            Dp, Ep = k.pb[2 * d_], k.pb[2 * d_ + 1]
                            k.tt(aT, View(Tm, bc(Tm.ap[:, d_, :], [128, 4, 128], 1)),
                                 View(smg, bc(smc(0, c, d_), [128, 4, 128], 2)), ALU.mult)
                            k.mm(Ep, onesF, View(aT, aT.ap.rearrange("p j t -> p (j t)")), True, True)
                            Ep3 = View(Ep, Ep.ap.rearrange("p (j t) -> p j t", j=4))
                            k.act(EB, Ep3, AF.Exp)
                            la, ea, ma = LT.ap, Ep3.ap, bc(maskn.ap[:, d_, :], [128, 4, 128], 1)
                            p.op("dve", lambda e, la=la, ea=ea, ma=ma: e.tensor_tensor(out=la, in0=ea, in1=ma, op=ALU.add),
                                 reads=[Ep, maskn, EB], writes=[LT])
                            for j in range(4):
                                k.act(LT[:, j, :], LT[:, j, :], AF.Exp,
                                      bias=View(smg, smg.ap[:, 1, c, d_ * 8 + hh * 4 + j:d_ * 8 + hh * 4 + j + 1]))
                            k.tt(MTs[d_], LT, View(GTs, bc(GTs.ap, [128, 4, 128], 1)), ALU.mult)
                            k.tt(CeTs[d_], EB, View(CT, bc(CT.ap[:, csl], [128, 4, 128], 1)), ALU.mult)
                            scaled_x(Xs.ap[:, d_, :], c, 5, d_)
                        yp = k.pb[4]
                        for j in range(4):
                            q = slice(j * 64, (j + 1) * 64)
                            k.mm(yp[:, q], MTs[0][:, j, :], Xs[:, 0, q], True, False)
                            k.mm(yp[:, q], CeTs[0][:, j, :], hbf[0][:, q], False, False)
                            k.mm(yp[:, q], MTs[1][:, j, :], Xs[:, 1, q], False, False)
                            k.mm(yp[:, q], CeTs[1][:, j, :], hbin[:, c, hh * 256 + j * 64:hh * 256 + (j + 1) * 64], False, True)
                        xv = xtok.ap[:, c, xsl].rearrange("p (j q) -> p j q", j=4)
                        k.tt(View(y2, y2.ap.rearrange("p (j q) -> p j q", j=4)), View(xtok, xv),
                             View(dsk, bc(dsk.ap[:, hs0:hs0 + 4], [128, 4, 64], 2)), ALU.mult)
                        k.tt(y2, y2, yp[:, 0:256], ALU.add)
                        ptf = k.pb[6]
                        for t2 in range(2):
                            pa, ia, ida = ptf.ap[:, 256 + t2 * 128:384 + t2 * 128], A(y2[:, t2 * 128:(t2 + 1) * 128]), identF.ap
                            p.op("pe", lambda e, pa=pa, ia=ia, ida=ida: e.transpose(pa, ia, ida),
                                 reads=[y2, identF], writes=[ptf], inc=(t2 == 1))
                        k.tt(xcT[:, hh * 2:hh * 2 + 2, csl], View(ptf, ptf.ap[:, 256:512].rearrange("p (a t) -> p a t", a=2)),
                             szT[:, hh * 2:hh * 2 + 2, csl], ALU.mult)
                        scaled_x(Xs.ap[:, 2, :], c, 4, 0)
                        state_update(0, c, Xs.ap[:, 2, :])
                    out_state(0, si)

            sqv = hbin.ap.rearrange("p c j -> p (c j)")[:, 0:2048].rearrange("p (a t) -> p a t", a=4)
            if sstage < 5:
                continue
            for blk in range(2):
                bsl = slice(blk * 512, (blk + 1) * 512)
                k.act(View(hbin, sqv), xcT[:, :, bsl], AF.Square)
                ps = nb()
                for t4 in range(4):
                    k.mm(ps, ones_bf, View(hbin, sqv[:, t4, :]), t4 == 0, t4 == 3)
                k.act(rstd, ps, AF.Sqrt, bias=epsb, scale=1.0 / 512.0)
                rs = A(rstd)
                p.op("dve", lambda e, rs=rs: e.reciprocal(out=rs, in_=rs), reads=[rstd], writes=[rstd])
                for t4 in range(4):
                    k.stt(xcT[:, t4, bsl], xcT[:, t4, bsl], ngrp[:, g * 4 + t4:g * 4 + t4 + 1], rstd, ALU.mult, ALU.mult)
            wo = wget2()
            wov = wo.ap.rearrange("p (h j) -> p h j", h=4)
            p.dma("pool", View(wo, wov), sw_out[g * 512:(g + 1) * 512, :].rearrange("(h p) j -> p h j", p=128))
            for dc in range(NCH):
                for blk in range(2):
                    ps = nb()
                    for t4 in range(4):
                        k.mm(ps, View(wo, wov[:, t4, dc * 128:(dc + 1) * 128]), xcT[:, t4, blk * 512:(blk + 1) * 512],
                             t4 == 0, t4 == 3)
                    xv_ = xT[:, dc, blk * 512:(blk + 1) * 512]
                    k.stt(xv_, ps, g1[:, dc:dc + 1], xv_, ALU.mult, ALU.add)
        p.alias(arB, bB)
        p.alias(arC, bC)
        p.alias(S2, convb + sweepb)

    for ps_ in range(2):
        if ps_ not in flags.get("passes", (0, 1)):
            continue
        ci = ps_
        for dc in range(NCH):
            p.dma("sp", xT[:, dc, :], x_in[ps_][:, dc, :])
        for l in flags.get("layers", (0, 1)):
            if l == 0 and has_attn:
                adaln(ci, 0, 0)
                attention(ci, ps_)
            if l == 1 and has_ssd:
                adaln(ci, 1, 0)
                ssd(ci, ps_)
            if flags.get("peer", True):
                if flags.get("dbg", 9) >= 1:
                    adaln(ci, l, 1)
                if flags.get("pphase", 9) > 0:
                    peer(ci, l)
        final_norm(y_out[ps_])
    p.emit()
    return nc


def fm(x2d):
    T, Dd = x2d.shape
    return np.ascontiguousarray(x2d.T.reshape(Dd // 128, 128, T).transpose(1, 0, 2))


def unfm(a):
    P, C, T = a.shape
    return np.ascontiguousarray(a.transpose(1, 0, 2).reshape(C * P, T).T)


def vec_pc(v):
    return np.ascontiguousarray(v.reshape(-1, 128).T)


_CACHE = {}


def prepare_inputs(inp, core):
    f = np.float32
    b = core // 2
    m = {}
    xp = inp["x_prompt"][4 * core:4 * core + 4].reshape(NT, D)
    m["xp"] = fm(xp)
    m["xs"] = fm(inp["x_sample"][b])
    m["cond"] = np.ascontiguousarray(np.stack([vec_pc(inp["c_ctx"]), vec_pc(inp["c"][b])], axis=-1)).astype(f)
    m["w_mod"] = inp["w_mod"]
    m["b_mod"] = np.ascontiguousarray(np.stack([vec_pc(inp["b_mod"][l]) for l in range(2)], axis=1)).astype(f)
    m["ng"] = np.ascontiguousarray(np.stack([vec_pc(inp["norm_g"][l, kk]) for l in range(2) for kk in range(2)], axis=1))
    m["fg"] = vec_pc(inp["final_g"])
    m["w_q"] = inp["peer_w_q"]
    m["skT"] = np.ascontiguousarray(inp["peer_sub_keys"].transpose(0, 4, 1, 2, 3).reshape(2, 128, 16, 128))
    m["uT"] = _CACHE["uT"]
    m["pv"] = inp["peer_v"]
    m["ident"] = np.eye(128, dtype=f)
    m["attn_w_in"] = inp["attn_w_in"][0]
    m["attn_w_out"] = inp["attn_w_out"][0]
    m["esink"] = np.ascontiguousarray(np.broadcast_to(inp["attn_sink"][0].reshape(1, 8), (128, 8))).astype(f)
    m["lamv"] = np.ascontiguousarray(np.broadcast_to(inp["diff_lam"][0][None], (128, 4, 128))).astype(f)
    m["subg"] = vec_pc(inp["diff_subln_g"][0])
    m["rope"] = _CACHE["rope"]
    m["rperm"] = _CACHE["rperm"]
    m["masks"] = _CACHE["masks"]
    m["caK"] = np.ascontiguousarray(inp["cache_a_k"][b, 0].transpose(2, 1, 0))
    m["caV"] = np.ascontiguousarray(inp["cache_a_v"][b, 0].reshape(4, 128, 2, 128).transpose(1, 0, 2, 3))
    m["cbK"] = np.ascontiguousarray(inp["cache_b_k"][b, 0].transpose(3, 1, 2, 0))
    m["cbV"] = np.ascontiguousarray(inp["cache_b_v"][b, 0].reshape(4, 128, 4, 256).transpose(1, 0, 2, 3))
    m["ssd_w_in"] = inp["ssd_w_in"][0]
    m["ssd_w_out"] = inp["ssd_w_out"][0]
    cw = inp["ssd_conv_w"][0]
    cb = inp["ssd_conv_b"][0]
    cwb = np.concatenate([cw, cb[None]], axis=0)
    m["convw"] = np.ascontiguousarray(cwb.reshape(6, 48, 128).transpose(2, 1, 0)).astype(f)
    m["dtbias"] = np.ascontiguousarray(np.broadcast_to(inp["ssd_dt_bias"][0].reshape(1, 128), (128, 128))).astype(f)
    m["alog"] = np.ascontiguousarray(np.broadcast_to(inp["ssd_a_log"][0].reshape(1, 128), (128, 128))).astype(f)
    m["dskip"] = np.ascontiguousarray(np.broadcast_to(inp["ssd_d"][0].reshape(1, 64), (128, 64))).astype(f)
    m["ngrp"] = vec_pc(inp["ssd_norm_g"][0])
    m["tri"] = _CACHE["tri"]
    m["st_f"] = np.ascontiguousarray(inp["state_ssd_fwd"][b, 0].transpose(2, 0, 1))
    m["st_b"] = np.ascontiguousarray(inp["state_ssd_bwd"][b, 0].transpose(2, 0, 1))
    return m


def _const_tables():
    f = np.float32
    L = 1024
    t = np.arange(L)
    row = (t // 64).astype(f)
    col = (t % 64).astype(f)
    inv = (10000.0 ** (-np.arange(0, 64, 2, dtype=f) / 64.0)).astype(f)
    ar = row[:, None] * inv
    ac = col[:, None] * inv
    ang = np.concatenate([ar, ar, ac, ac], axis=-1)
    cos = np.cos(ang).astype(f).T
    sin = np.sin(ang).astype(f).T
    sign = np.ones((128, 1), f)
    sign[0:32] = -1.0
    sign[64:96] = -1.0
    _CACHE["rope"] = np.ascontiguousarray(np.stack([cos, sin * sign], axis=0))
    perm = np.arange(128)
    perm = np.where((perm % 64) < 32, perm + 32, perm - 32)
    R = np.zeros((128, 128), f)
    R[perm, np.arange(128)] = 1.0
    _CACHE["rperm"] = R
    kk = np.arange(128)[:, None]
    qq = np.arange(128)[None, :]
    mp = np.where(kk >= qq, 0.0, -30000.0).astype(f)
    mn = np.where(kk <= qq, 0.0, -30000.0).astype(f)
    tf = (kk <= qq).astype(f)
    tb = (kk >= qq).astype(f)
    _CACHE["tri"] = np.ascontiguousarray(np.stack([tf, tb, (tf - 1.0) * 30000.0, (tb - 1.0) * 30000.0], axis=0))
    _CACHE["masks"] = np.ascontiguousarray(np.stack([np.tile(mp, (1, 4)), np.tile(mn, (1, 4))], axis=0))


def kernel(**inp):
    inp = {k_: np.asarray(v) for k_, v in inp.items()}
    flags = _CACHE.get("flags", {})
    nc = build_program(flags)
    _CACHE["uT"] = np.ascontiguousarray(inp["peer_u"].transpose(0, 2, 1))
    _const_tables()
    in_maps = [prepare_inputs(inp, c) for c in range(8)]
    res = run_bass_kernel_spmd(nc, in_maps, core_ids=list(range(8)))
    r = res.results
    f = np.float32
    y_prompt = np.concatenate([unfm(r[c]["yp"]).reshape(4, 256, D) for c in range(8)], axis=0).astype(f)
    y_sample = np.stack([unfm(r[2 * b]["ys"]) for b in range(4)], axis=0).astype(f)
    cat = lambda name, shp: np.concatenate([np.asarray(r[c][name]).reshape((4,) + shp) for c in range(8)], axis=0).astype(f)
    new_a_k = cat("na_k", (1, 256, 2, 128))
    new_a_v = cat("na_v", (1, 256, 2, 128))
    new_b_k = cat("nb_k", (1, 256, 4, 2, 128))
    new_b_v = cat("nb_v", (1, 256, 4, 256))
    new_ssd_fwd = cat("ns_f", (1, 64, 64, 128))
    new_ssd_bwd = cat("ns_b", (1, 64, 64, 128))
    return (y_prompt, y_sample, new_a_k, new_a_v, new_b_k, new_b_v, new_ssd_fwd, new_ssd_bwd)
```
